# Optimizing a Trainium2 kernel written in Bass

```python
import math
import jax
import jax.numpy as jnp
from jax import lax
import numpy as np

D_MODEL = 2048
BATCH = 2
SEQ = 16384
DEPTH = 2

GRID_W = 64
HEAD_DIM = 128
NA_HEADS = 4
NA_WIN_R = 8
NA_WIN_C = 16
DA_HEADS = 4
DA_QK_DIM = 64
DA_V_DIM = 2 * DA_QK_DIM
DA_BLOCK = 128
GDN_HEADS = 8
GDN_DK = 128
GDN_DV = 128
GDN_CONV = 5
GDN_CHUNK = 64
N_BRANCHES = 3
FFN_HIDDEN = (((8 * D_MODEL + 2) // 3 + 255) // 256) * 256
RMS_EPS = 1e-6

NA_W = NA_HEADS * HEAD_DIM
DA_QK_W = DA_HEADS * 2 * DA_QK_DIM
DA_V_W = DA_HEADS * DA_V_DIM
GDN_K_W = GDN_HEADS * GDN_DK
GDN_V_W = GDN_HEADS * GDN_DV
GDN_CONV_CH = 2 * GDN_K_W + GDN_V_W
IN_SIZES = (NA_W, NA_W, NA_W, DA_QK_W, DA_QK_W, DA_V_W, GDN_K_W, GDN_K_W, GDN_V_W, GDN_V_W, 2 * GDN_HEADS, 2 * GDN_HEADS)
IN_WIDTH = sum(IN_SIZES)

kernel_name = "hybrid_natten_diffattn_gdn_block"


def _rms(x, g):
    xf = x.astype(jnp.float32)
    y = xf * lax.rsqrt(jnp.mean(xf * xf, axis=-1, keepdims=True) + RMS_EPS)
    return (y * g.astype(jnp.float32)).astype(x.dtype)


def _l2norm(x):
    xf = x.astype(jnp.float32)
    return xf * lax.rsqrt(jnp.sum(xf * xf, axis=-1, keepdims=True) + RMS_EPS)


def _split_cols(u):
    parts, start = [], 0
    for size in IN_SIZES:
        parts.append(u[..., start:start + size])
        start += size
    return parts


def _neighbourhood_attention(q, k, v, rpb):
    b, s, h, d = q.shape
    rows = s // GRID_W
    kr = min(NA_WIN_R, rows)
    kc = NA_WIN_C
    qg = q.reshape(b, rows, GRID_W, h, d)
    kg = k.reshape(b, rows, GRID_W, h, d)
    vg = v.reshape(b, rows, GRID_W, h, d)
    col = jnp.arange(GRID_W)
    col_start = jnp.clip(col - kc // 2, 0, GRID_W - kc)
    col_idx = col_start[:, None] + jnp.arange(kc)[None, :]
    col_bias_idx = col_idx - col[:, None] + (NA_WIN_C - 1)
    scale = d ** -0.5

    def row_block(args):
        q_row, r = args
        r_start = jnp.clip(r - kr // 2, 0, rows - kr)
        k_band = lax.dynamic_slice_in_dim(kg, r_start, kr, axis=1)
        v_band = lax.dynamic_slice_in_dim(vg, r_start, kr, axis=1)
        k_nb = k_band[:, :, col_idx]
        v_nb = v_band[:, :, col_idx]
        row_bias_idx = r_start + jnp.arange(kr) - r + (NA_WIN_R - 1)
        bias = rpb[:, row_bias_idx][:, :, col_bias_idx]
        sc = jnp.einsum('bwhd,brwchd->bhwrc', q_row, k_nb).astype(jnp.float32) * scale
        sc = sc + bias.transpose(0, 2, 1, 3)[None].astype(jnp.float32)
        p = jax.nn.softmax(sc.reshape(b, h, GRID_W, kr * kc), axis=-1).reshape(b, h, GRID_W, kr, kc)
        return jnp.einsum('bhwrc,brwchd->bwhd', p.astype(v.dtype), v_nb)

    out = lax.map(row_block, (jnp.moveaxis(qg, 1, 0), jnp.arange(rows)))
    return jnp.moveaxis(out, 0, 1).reshape(b, s, h, d)


def _diff_attention(q, k, v, lam, lam_init, subln_g):
    b, s, h, _, dq = q.shape
    nblk = s // DA_BLOCK
    scale = dq ** -0.5
    slopes = jnp.asarray([2.0 ** (-8.0 * (i + 1) / h) for i in range(h)], dtype=jnp.float32)
    kpos = jnp.arange(s)
    qb = jnp.moveaxis(q.reshape(b, nblk, DA_BLOCK, h, 2, dq), 1, 0)

    def block(args):
        q_blk, i = args
        qpos = i * DA_BLOCK + jnp.arange(DA_BLOCK)
        dist = jnp.abs(qpos[:, None] - kpos[None, :]).astype(jnp.float32)
        alibi = -slopes[:, None, None] * dist[None]
        sc = jnp.einsum('bqhcd,bkhcd->bhcqk', q_blk, k).astype(jnp.float32) * scale + alibi[None, :, None]
        p = jax.nn.softmax(sc, axis=-1)
        a = p[:, :, 0] - lam * p[:, :, 1]
        return jnp.einsum('bhqk,bkhd->bqhd', a.astype(v.dtype), v)

    o = lax.map(block, (qb, jnp.arange(nblk)))
    o = jnp.moveaxis(o, 0, 1).reshape(b, s, h, v.shape[-1])
    return _rms(o, subln_g) * (1.0 - lam_init)


def _short_conv(x, w):
    ch = x.shape[-1]
    return lax.conv_general_dilated(
        x, w[:, None, :].astype(x.dtype), window_strides=(1,),
        padding=[(GDN_CONV // 2, GDN_CONV // 2)],
        dimension_numbers=('NWC', 'WIO', 'NWC'), feature_group_count=ch)


def _chunk_gated_delta(q, k, v, beta, g):
    b, h, s, dk = q.shape
    dv = v.shape[-1]
    C = GDN_CHUNK
    n = s // C
    q = q.reshape(b, h, n, C, dk)
    k = k.reshape(b, h, n, C, dk)
    v = v.reshape(b, h, n, C, dv)
    beta = beta.reshape(b, h, n, C)
    gc = jnp.cumsum(g.reshape(b, h, n, C), axis=-1)
    tri_incl = jnp.tril(jnp.ones((C, C), dtype=bool))
    tri_strict = jnp.tril(jnp.ones((C, C), dtype=bool), -1)
    decay = jnp.exp(jnp.where(tri_incl, gc[..., :, None] - gc[..., None, :], -jnp.inf))
    kb = k * beta[..., None]
    a = jnp.where(tri_strict, jnp.einsum('bhnid,bhnjd->bhnij', kb, k) * decay, 0.0)
    eye = jnp.eye(C, dtype=jnp.float32)
    t = lax.linalg.triangular_solve(eye + a, jnp.broadcast_to(eye, a.shape),
                                    left_side=True, lower=True, unit_diagonal=True)
    u = jnp.einsum('bhnij,bhnjd->bhnid', t, v * beta[..., None])
    w = jnp.einsum('bhnij,bhnjd->bhnid', t, kb * jnp.exp(gc)[..., None])
    qg = q * jnp.exp(gc)[..., None]
    a_intra = jnp.einsum('bhnid,bhnjd->bhnij', q, k) * decay
    k_tail = k * jnp.exp(gc[..., -1:] - gc)[..., None]
    g_last = jnp.exp(gc[..., -1])

    def step(state, inp):
        u_i, w_i, qg_i, ai_i, kt_i, gl_i = inp
        v_new = u_i - jnp.einsum('bhcd,bhde->bhce', w_i, state)
        o = jnp.einsum('bhcd,bhde->bhce', qg_i, state) + jnp.einsum('bhij,bhje->bhie', ai_i, v_new)
        state = state * gl_i[..., None, None] + jnp.einsum('bhcd,bhce->bhde', kt_i, v_new)
        return state, o

    xs = tuple(jnp.moveaxis(z, 2, 0) for z in (u, w, qg, a_intra, k_tail, g_last))
    _, o = lax.scan(step, jnp.zeros((b, h, dk, dv), jnp.float32), xs)
    return jnp.moveaxis(o, 0, 2).reshape(b, h, s, dv)


def _gated_deltanet(q, k, v, z, beta_logit, alpha_logit, conv_w, a_log, dt_bias, norm_g):
    b, s, _ = q.shape
    hh = GDN_HEADS
    qkv = jax.nn.silu(_short_conv(jnp.concatenate([q, k, v], axis=-1), conv_w))
    q, k, v = qkv[..., :GDN_K_W], qkv[..., GDN_K_W:2 * GDN_K_W], qkv[..., 2 * GDN_K_W:]
    q = (_l2norm(q.reshape(b, s, hh, GDN_DK)) * GDN_DK ** -0.5).transpose(0, 2, 1, 3)
    k = _l2norm(k.reshape(b, s, hh, GDN_DK)).transpose(0, 2, 1, 3)
    v = v.reshape(b, s, hh, GDN_DV).astype(jnp.float32).transpose(0, 2, 1, 3)
    beta = jax.nn.sigmoid(beta_logit.astype(jnp.float32).reshape(b, s, 2, hh))
    g = -jnp.exp(a_log.astype(jnp.float32)) * jax.nn.softplus(
        alpha_logit.astype(jnp.float32).reshape(b, s, 2, hh) + dt_bias.astype(jnp.float32))
    beta = beta.transpose(2, 0, 3, 1)
    g = g.transpose(2, 0, 3, 1)
    o_f = _chunk_gated_delta(q, k, v, beta[0], g[0])
    o_b = jnp.flip(_chunk_gated_delta(jnp.flip(q, 2), jnp.flip(k, 2), jnp.flip(v, 2),
                                      jnp.flip(beta[1], -1), jnp.flip(g[1], -1)), 2)
    o = (o_f + o_b).transpose(0, 2, 1, 3).astype(z.dtype)
    o = _rms(o, norm_g) * jax.nn.silu(z.reshape(b, s, hh, GDN_DV))
    return o.reshape(b, s, GDN_V_W)


def setup_inputs(seed: int = 0) -> dict:
    key = jax.random.key(seed)
    ks = jax.random.split(key, 32)
    f32 = jnp.float32
    L, D = DEPTH, D_MODEL

    def nrm(k, shape, scale):
        return jax.random.normal(k, shape, f32) * scale

    dt = jnp.exp(jax.random.uniform(ks[18], (L, 2, GDN_HEADS), f32, math.log(1e-3), math.log(1e-1)))
    return {
        "x": nrm(ks[0], (BATCH, SEQ, D), 1.0),
        "c": nrm(ks[1], (BATCH, D), 1.0),
        "ada_w": nrm(ks[2], (L, D, 6 * D), 0.5 * D ** -0.5),
        "ada_b": nrm(ks[3], (L, 6 * D), 0.02),
        "norm1_g": 1.0 + nrm(ks[4], (L, D), 0.02),
        "norm2_g": 1.0 + nrm(ks[5], (L, D), 0.02),
        "w_in": nrm(ks[6], (L, D, IN_WIDTH), D ** -0.5),
        "gate_w": nrm(ks[7], (L, D, N_BRANCHES * D), D ** -0.5),
        "gate_b": nrm(ks[8], (L, N_BRANCHES * D), 0.1),
        "na_qnorm_g": 1.0 + nrm(ks[9], (L, HEAD_DIM), 0.02),
        "na_knorm_g": 1.0 + nrm(ks[10], (L, HEAD_DIM), 0.02),
        "na_rpb": nrm(ks[11], (L, NA_HEADS, 2 * NA_WIN_R - 1, 2 * NA_WIN_C - 1), 0.1),
        "da_qnorm_g": 1.0 + nrm(ks[12], (L, DA_QK_DIM), 0.02),
        "da_knorm_g": 1.0 + nrm(ks[13], (L, DA_QK_DIM), 0.02),
        "da_lambda": nrm(ks[14], (L, 4, DA_QK_DIM), 0.1),
        "da_subln_g": 1.0 + nrm(ks[15], (L, DA_V_DIM), 0.02),
        "gdn_conv_w": nrm(ks[16], (L, GDN_CONV, GDN_CONV_CH), GDN_CONV ** -0.5),
        "gdn_a_log": jnp.log(jax.random.uniform(ks[17], (L, 2, GDN_HEADS), f32, 1.0, 16.0)),
        "gdn_dt_bias": dt + jnp.log(-jnp.expm1(-dt)),
        "gdn_norm_g": 1.0 + nrm(ks[19], (L, GDN_DV), 0.02),
        "w_branch_a": nrm(ks[20], (L, NA_W, D), NA_W ** -0.5),
        "w_branch_b": nrm(ks[21], (L, DA_V_W, D), DA_V_W ** -0.5),
        "w_branch_c": nrm(ks[22], (L, GDN_V_W, D), GDN_V_W ** -0.5),
        "w_out": nrm(ks[23], (L, D, D), D ** -0.5),
        "ffn_w1": nrm(ks[24], (L, D, FFN_HIDDEN), D ** -0.5),
        "ffn_w3": nrm(ks[25], (L, D, FFN_HIDDEN), D ** -0.5),
        "ffn_w2": nrm(ks[26], (L, FFN_HIDDEN, D), FFN_HIDDEN ** -0.5),
    }


def reference(x, c, ada_w, ada_b, norm1_g, norm2_g, w_in, gate_w, gate_b,
              na_qnorm_g, na_knorm_g, na_rpb, da_qnorm_g, da_knorm_g, da_lambda, da_subln_g,
              gdn_conv_w, gdn_a_log, gdn_dt_bias, gdn_norm_g,
              w_branch_a, w_branch_b, w_branch_c, w_out, ffn_w1, ffn_w3, ffn_w2):
    b, s, d = x.shape
    for l in range(DEPTH):
        mod = jax.nn.silu(c) @ ada_w[l] + ada_b[l]
        sh1, sc1, gt1, sh2, sc2, gt2 = jnp.split(mod[:, None, :], 6, axis=-1)
        h = _rms(x, norm1_g[l]) * (1.0 + sc1) + sh1
        (na_q, na_k, na_v, da_q, da_k, da_v,
         gd_q, gd_k, gd_v, gd_z, gd_beta, gd_alpha) = _split_cols(h @ w_in[l])

        ya = _neighbourhood_attention(
            _rms(na_q.reshape(b, s, NA_HEADS, HEAD_DIM), na_qnorm_g[l]),
            _rms(na_k.reshape(b, s, NA_HEADS, HEAD_DIM), na_knorm_g[l]),
            na_v.reshape(b, s, NA_HEADS, HEAD_DIM), na_rpb[l]).reshape(b, s, NA_W)

        lam_init = 0.8 - 0.6 * math.exp(-0.3 * l)
        lp = da_lambda[l].astype(jnp.float32)
        lam = jnp.exp(jnp.sum(lp[0] * lp[1])) - jnp.exp(jnp.sum(lp[2] * lp[3])) + lam_init
        yb = _diff_attention(
            _rms(da_q.reshape(b, s, DA_HEADS, 2, DA_QK_DIM), da_qnorm_g[l]),
            _rms(da_k.reshape(b, s, DA_HEADS, 2, DA_QK_DIM), da_knorm_g[l]),
            da_v.reshape(b, s, DA_HEADS, DA_V_DIM), lam, lam_init, da_subln_g[l]).reshape(b, s, DA_V_W)

        yc = _gated_deltanet(gd_q, gd_k, gd_v, gd_z, gd_beta, gd_alpha,
                             gdn_conv_w[l], gdn_a_log[l], gdn_dt_bias[l], gdn_norm_g[l])

        gates = jax.nn.sigmoid(h @ gate_w[l] + gate_b[l]).reshape(b, s, N_BRANCHES, d)
        merged = (gates[:, :, 0] * (ya @ w_branch_a[l])
                  + gates[:, :, 1] * (yb @ w_branch_b[l])
                  + gates[:, :, 2] * (yc @ w_branch_c[l]))
        x = x + gt1 * (merged @ w_out[l])

        h2 = _rms(x, norm2_g[l]) * (1.0 + sc2) + sh2
        x = x + gt2 * ((jax.nn.silu(h2 @ ffn_w1[l]) * (h2 @ ffn_w3[l])) @ ffn_w2[l])
    return x
```

```python
import math
import numpy as np
from contextlib import ExitStack
import concourse.bass as bass
import concourse.mybir as mybir
from concourse.bass_utils import run_bass_kernel_spmd


F32 = mybir.dt.float32
BF16 = mybir.dt.bfloat16
AF = mybir.ActivationFunctionType
ALU = mybir.AluOpType
AX = mybir.AxisListType

ENGS = ("pe", "dve", "act", "pool", "sp")


class Buf:
    __slots__ = ("name", "w", "r", "dsem", "dcnt", "dwr", "drd")

    def __init__(self, name):
        self.name = name
        self.w = None
        self.r = {}
        self.dsem = None
        self.dcnt = 0
        self.dwr = {}
        self.drd = {}


class Sched:
    def __init__(self, nc, stack):
        self.nc = nc
        self.stack = stack
        self.q = {e: [] for e in ENGS}
        self.cnt = {e: 0 for e in ENGS}
        self.sem = {e: stack.enter_context(nc.semaphore("s_" + e)) for e in ENGS}
        self.seen = {e: {} for e in ENGS}
        self.nbuf = 0
        self.ndsem = 0
        self.dissued = {}

    def buf(self, name=None):
        self.nbuf += 1
        return Buf(name or ("b%d" % self.nbuf))

    def _dsem(self, b):
        if b.dsem is None:
            self.ndsem += 1
            b.dsem = self.stack.enter_context(self.nc.semaphore("d_%s_%d" % (b.name, self.ndsem)))
        return b.dsem

    def _wait(self, eng, key, sem, val):
        if val <= 0:
            return
        if self.seen[eng].get(key, 0) >= val:
            return
        self.seen[eng][key] = val
        self.q[eng].append(lambda e, sem=sem, val=val: e.wait_ge(sem, val))

    def _deps(self, eng, reads, writes, is_dma=False):
        for b in reads:
            if b.w is not None:
                e2, c2 = b.w
                if is_dma or not (e2 == eng and eng == "pe"):
                    self._wait(eng, e2, self.sem[e2], c2)
            for sem, val in b.dwr.items():
                self._wait(eng, id(sem), sem, val)
        for b in writes:
            if b.w is not None:
                e2, c2 = b.w
                if is_dma or not (e2 == eng and eng == "pe"):
                    self._wait(eng, e2, self.sem[e2], c2)
            for e2, c2 in b.r.items():
                if is_dma or not (e2 == eng and eng == "pe"):
                    self._wait(eng, e2, self.sem[e2], c2)
            for sem, val in b.dwr.items():
                self._wait(eng, id(sem), sem, val)
            for sem, val in b.drd.items():
                self._wait(eng, id(sem), sem, val)

    def op(self, eng, fn, reads=(), writes=()):
        self._deps(eng, reads, writes)
        self.cnt[eng] += 1
        c = self.cnt[eng]
        sem = self.sem[eng]
        self.q[eng].append(lambda e, fn=fn, sem=sem: fn(e).then_inc(sem, 1))
        for b in reads:
            b.r[eng] = c
        for b in writes:
            b.w = (eng, c)
            b.r = {}
            b.dwr = {}
            b.drd = {}
        return c

    def dma(self, eng, out, in_, reads=(), writes=(), transpose=False, **kw):
        self._deps(eng, reads, writes, is_dma=True)
        prim = writes[0] if len(writes) else reads[0]
        sem = self._dsem(prim)
        prim.dcnt += 1
        val = 16 * prim.dcnt
        self.dissued[id(sem)] = (sem, val)
        if transpose:
            self.q[eng].append(lambda e, out=out, in_=in_, sem=sem: e.dma_start_transpose(out=out, in_=in_).then_inc(sem, 16))
        else:
            self.q[eng].append(lambda e, out=out, in_=in_, sem=sem, kw=kw: e.dma_start(out=out, in_=in_, **kw).then_inc(sem, 16))
        for b in reads:
            b.drd[sem] = val
        for b in writes:
            b.dwr = {sem: val}
            b.drd = {}
            b.w = None
            b.r = {}

    def barrier(self):
        for e in ENGS:
            for e2 in ENGS:
                self._wait(e, e2, self.sem[e2], self.cnt[e2])
            for sem, val in self.dissued.values():
                self._wait(e, id(sem), sem, val)

    def wait_all_dma(self, eng, bufs):
        for b in bufs:
            for sem, val in list(b.dwr.items()) + list(b.drd.items()):
                self._wait(eng, id(sem), sem, val)

    def flush(self):
        nc = self.nc
        q = self.q
        with nc.Block() as block:
            @block.tensor
            def _(e):
                for f in q["pe"]:
                    f(e)

            @block.vector
            def _(e):
                for f in q["dve"]:
                    f(e)

            @block.scalar
            def _(e):
                for f in q["act"]:
                    f(e)

            @block.gpsimd
            def _(e):
                for f in q["pool"]:
                    f(e)

            @block.sync
            def _(e):
                for f in q["sp"]:
                    f(e)
        self.q = {e: [] for e in ENGS}


def col16(v):
    return np.ascontiguousarray(v.reshape(-1, 128).T)

def prep_G_weights(l, inp):
    gate_w = inp["gate_w"][l]
    w_br = np.concatenate([inp["w_branch_a"][l], inp["w_branch_b"][l], inp["w_branch_c"][l]], axis=0)
    g5 = gate_w.reshape(16, 128, 3, 16, 128)
    g5 = g5.transpose(3, 1, 0, 2, 4).reshape(16, 128, 16 * 3 * 128)
    b4 = w_br.reshape(16, 128, 16, 128).transpose(2, 1, 0, 3).reshape(16, 128, 16 * 128)
    gwb = np.ascontiguousarray(np.concatenate([g5, b4], axis=2))
    wo = inp["w_out"][l].reshape(16, 128, 4, 512).transpose(2, 1, 0, 3).reshape(4, 128, 8192)
    w1 = inp["ffn_w1"][l].reshape(16, 128, 22, 256).transpose(2, 1, 0, 3).reshape(22, 128, 4096)
    w3 = inp["ffn_w3"][l].reshape(16, 128, 22, 256).transpose(2, 1, 0, 3).reshape(22, 128, 4096)
    w13 = np.ascontiguousarray(np.concatenate([w1, w3], axis=2))
    w2 = inp["ffn_w2"][l].reshape(4, 11, 128, 4, 512).transpose(3, 0, 2, 1, 4).reshape(16, 128, 5632)
    return dict(
        gwb=gwb, wo_r=np.ascontiguousarray(wo), w13=w13, w2r=np.ascontiguousarray(w2),
        ada_w=np.ascontiguousarray(inp["ada_w"][l]), ada_b=np.ascontiguousarray(inp["ada_b"][l][None, :]),
        n1g=col16(inp["norm1_g"][l]), n2g=col16(inp["norm2_g"][l]),
        gate_b=np.ascontiguousarray(inp["gate_b"][l].reshape(48, 128).T),
    )


NEGB = -30000.0


def a_plan(R):
    NP = R // 2
    plan = []
    nxt = 5
    for rp in range(NP):
        rows = [2 * rp, 2 * rp + 1]
        rs = [min(max(r - 4, 0), R - 8) for r in rows]
        lo = min(rs)
        hi = max(rs) + 7
        kts = list(range(lo // 2, hi // 2 + 1))
        interior = (rs[0] == rows[0] - 4) and (rs[1] == rows[1] - 4)
        ent = []
        for kt in kts:
            if interior:
                ent.append((kt, kt - rp + 2))
            else:
                ent.append((kt, nxt))
                nxt += 1
        plan.append(ent)
    return plan, nxt


def a_bias_table(rpb_h, R):
    plan, nb = a_plan(R)
    tab = np.full((nb, 128, 128), NEGB, np.float32)
    done = set()
    kk = np.arange(128)
    krl, kc = kk // 64, kk % 64
    for rp, ent in enumerate(plan):
        for kt, idx in ent:
            if idx in done:
                continue
            done.add(idx)
            for ql in range(2):
                r = 2 * rp + ql
                rs = min(max(r - 4, 0), R - 8)
                for qc in range(64):
                    cs = min(max(qc - 8, 0), 64 - 16)
                    keyrow = 2 * kt + krl
                    valid = (keyrow >= rs) & (keyrow < rs + 8) & (kc >= cs) & (kc < cs + 16)
                    ri = keyrow - r + 7
                    ci = kc - qc + 15
                    vals = rpb_h[np.clip(ri, 0, 14), np.clip(ci, 0, 30)]
                    col = np.where(valid, vals, NEGB).astype(np.float32)
                    tab[idx, :, ql * 64 + qc] = col
    return np.ascontiguousarray(tab.transpose(1, 0, 2))


def b_consts(j):
    m = 2.0 ** (-2.0 * (j + 1))
    kr = np.arange(128, dtype=np.float64)
    dl = np.arange(128, dtype=np.float64)
    tbb = (m * kr[:, None] - m * 128 * dl[None, :]).astype(np.float32)
    tba = (-m * kr[:, None] - m * 128 * dl[None, :]).astype(np.float32)
    qr = np.arange(512, dtype=np.float64)
    bov = np.zeros((128, 4, 512), np.float32)
    for jj in range(4):
        bov[:, jj, :] = -m * np.abs(qr[None, :] - (128 * jj + kr[:, None]))
    a = np.floor(qr / 128)
    b = qr - 128 * a
    aug = np.stack([-m * 128 * a, -m * b, m * 128 * a, m * b]).astype(np.float32)
    return tbb, tba, bov, aug


def c_consts():
    p = np.arange(128)
    blk = p // 64
    same = blk[:, None] == blk[None, :]
    c = np.zeros((13, 128, 128), np.float32)
    c[0] = np.eye(128)
    c[1] = 1.0
    c[2] = same
    c[3] = -1.0 * same
    c[4] = same & (p[:, None] <= p[None, :])
    c[5] = same & (p[:, None] >= p[None, :])
    inc = same & (p[None, :] <= p[:, None])
    c[6] = np.where(inc, 0.0, -1e4)
    c[7] = np.where(inc.T, 0.0, -1e4)
    st = same & (p[None, :] < p[:, None])
    c[8] = st
    c[9] = st.T
    c[10] = (p[:, None] < 64) * np.ones((1, 128))
    c[11] = (p[:, None] >= 64) * np.ones((1, 128))
    c[12, :, 0] = p < 64
    c[12, :, 1] = p >= 64
    return np.ascontiguousarray(c.transpose(1, 0, 2))


def f_cols(j):
    A0, B0, C0 = 0, 1536, 3072
    cols = []
    cols += list(range(A0 + j * 128, A0 + (j + 1) * 128))
    cols += list(range(A0 + 512 + j * 128, A0 + 512 + (j + 1) * 128))
    cols += list(range(A0 + 1024 + j * 128, A0 + 1024 + (j + 1) * 128))
    cols += list(range(B0 + j * 128, B0 + (j + 1) * 128))
    cols += list(range(B0 + 512 + j * 128, B0 + 512 + (j + 1) * 128))
    cols += list(range(B0 + 1024 + j * 128, B0 + 1024 + (j + 1) * 128))
    for h in (2 * j, 2 * j + 1):
        for t in range(3):
            cols += list(range(C0 + t * 1024 + h * 128, C0 + t * 1024 + (h + 1) * 128))
    for h in (2 * j, 2 * j + 1):
        cols += list(range(C0 + 3072 + h * 128, C0 + 3072 + (h + 1) * 128))
    bb = C0 + 4096
    for d in range(2):
        for h in (2 * j, 2 * j + 1):
            cols.append(bb + d * 8 + h)
    for d in range(2):
        for h in (2 * j, 2 * j + 1):
            cols.append(bb + 16 + d * 8 + h)
    return np.array(cols)


def prep_F(l, inp, j, S, lam_init):
    R = S // 64
    cols = f_cols(j)
    w = inp["w_in"][l][:, cols]
    w_r = np.ascontiguousarray(w.reshape(16, 128, -1).transpose(1, 0, 2))
    gq = np.stack([inp["na_qnorm_g"][l], inp["na_knorm_g"][l],
                   np.tile(inp["da_qnorm_g"][l], 2), np.tile(inp["da_knorm_g"][l], 2)], axis=1).astype(np.float32)
    gsc = np.tile(np.array([[128 ** -0.5, 1.0, 64 ** -0.5, 1.0]], np.float32), (128, 1))
    tbb, tba, bov, aug = b_consts(j)
    cw = inp["gdn_conv_w"][l]
    convw = np.zeros((128, 6, 5), np.float32)
    for hl, h in enumerate((2 * j, 2 * j + 1)):
        for t in range(3):
            convw[:, hl * 3 + t, :] = cw[:, t * 1024 + h * 128:t * 1024 + (h + 1) * 128].T
    hs = [2 * j, 2 * j + 1]
    alog = np.array([[inp["gdn_a_log"][l][d, h] for d in range(2) for h in hs]], np.float32)
    dtb = np.array([[inp["gdn_dt_bias"][l][d, h] for d in range(2) for h in hs]], np.float32)
    return dict(
        w_in=w_r, gq=gq, gsc=gsc,
        ada_w=np.ascontiguousarray(inp["ada_w"][l][:, :4096]), ada_b=np.ascontiguousarray(inp["ada_b"][l][None, :4096]),
        n1g=col16(inp["norm1_g"][l]),
        biasA=a_bias_table(inp["na_rpb"][l][j], R),
        tbb=tbb, tba=tba, bov=bov, aug=aug,
        dal=np.ascontiguousarray(inp["da_lambda"][l].reshape(1, 256)),
        lamc=np.tile(np.array([[lam_init, 1.0 - lam_init]], np.float32), (128, 1)),
        subg=np.ascontiguousarray(inp["da_subln_g"][l][None, :]),
        convw=convw, alog=alog, dtb=dtb,
        gng=np.ascontiguousarray(inp["gdn_norm_g"][l][None, :]),
        cst=c_consts(),
    )


D = 2048
KC = 16
FH = 5632
JC = 44
EPS = 1e-6


def mod_rows(S, nc, st, c_col, ada_w, ada_b, mod_d, b_mod_d, ncols, pool):
    lst = st
    GW = 1024
    ccol = lst.enter_context(nc.sbuf_tensor("ccol", [128, KC], F32))
    sg = lst.enter_context(nc.sbuf_tensor("csg", [128, KC], F32))
    scol = lst.enter_context(nc.sbuf_tensor("scol", [128, KC], F32))
    brow = lst.enter_context(nc.sbuf_tensor("brow", [1, GW], F32))
    mrow = lst.enter_context(nc.sbuf_tensor("mrow", [1, GW], F32))
    wt = [lst.enter_context(nc.sbuf_tensor("adaw%d" % i, [128, GW], F32)) for i in range(3)]
    b_c, b_s, b_br, b_mr = S.buf(), S.buf(), S.buf(), S.buf()
    b_wt = [S.buf() for _ in range(3)]
    S.dma("sp", ccol[:], c_col, writes=[b_c])
    S.op("act", lambda e: e.activation(sg[:], ccol[:], AF.Sigmoid), reads=[b_c], writes=[b_s])
    S.op("dve", lambda e: e.tensor_mul(scol[:], sg[:], ccol[:]), reads=[b_s, b_c], writes=[b_s])
    ps = pool
    it = 0
    for g in range(ncols // GW):
        S.dma("sp", brow[:], ada_b[:, g * GW:(g + 1) * GW], writes=[b_br])
        for k in range(KC):
            slot = it % 3
            S.dma("sp" if it % 2 == 0 else "act", wt[slot][:], ada_w[k * 128:(k + 1) * 128, g * GW:(g + 1) * GW], writes=[b_wt[slot]])
            for n in range(2):
                pt, bpt = ps[n]
                S.op("pe", lambda e, pt=pt, slot=slot, k=k, n=n: e.matmul(pt[0:1, :], scol[:, k:k + 1], wt[slot][:, n * 512:(n + 1) * 512], start=(k == 0), stop=(k == KC - 1)),
                     reads=[b_s, b_wt[slot]], writes=[bpt])
            it += 1
        for n in range(2):
            pt, bpt = ps[n]
            S.op("dve", lambda e, pt=pt, n=n: e.tensor_add(mrow[:, n * 512:(n + 1) * 512], pt[0:1, :], brow[:, n * 512:(n + 1) * 512]), reads=[bpt, b_br], writes=[b_mr])
        S.dma("sp", mod_d[:, g * GW:(g + 1) * GW], mrow[:], reads=[b_mr], writes=[b_mod_d])


def build_G(NT, debug=False):
    nc = bass.Bass("TRN2", target_bir_lowering=False)
    dt = nc.dram_tensor
    x_d = dt("x", [NT, D], F32, kind="ExternalInput").ap()
    y_d = dt("y", [NT, D], BF16, kind="ExternalInput").ap()
    c_col = dt("c_col", [128, KC], F32, kind="ExternalInput").ap()
    ada_w = dt("ada_w", [D, 6 * D], F32, kind="ExternalInput").ap()
    ada_b = dt("ada_b", [1, 6 * D], F32, kind="ExternalInput").ap()
    n1g = dt("n1g", [128, KC], F32, kind="ExternalInput").ap()
    n2g = dt("n2g", [128, KC], F32, kind="ExternalInput").ap()
    gwb = dt("gwb", [KC, 128, 8192], F32, kind="ExternalInput").ap()
    gate_b = dt("gate_b", [128, 48], F32, kind="ExternalInput").ap()
    wo_r = dt("wo_r", [4, 128, 8192], F32, kind="ExternalInput").ap()
    w13 = dt("w13", [JC // 2, 128, 8192], F32, kind="ExternalInput").ap()
    w2r = dt("w2r", [16, 128, 5632], F32, kind="ExternalInput").ap()
    out_d = dt("out", [NT, D], F32, kind="ExternalOutput").ap()
    mod_d = dt("mod_d", [1, 6 * D], F32).ap()

    dbg = {}

    def dump(S, name, ap, shape, dtype, bufs):
        if not debug:
            return
        d = dt("dbg_" + name, shape, dtype, kind="ExternalOutput").ap()
        S.dma("sp", d, ap, reads=bufs)
        dbg[name] = bufs

    with ExitStack() as st:
        S = Sched(nc, st)
        sb = lambda name, shape, dtype: st.enter_context(nc.sbuf_tensor(name, shape, dtype))
        ps = []
        for i in range(6):
            ps.append((st.enter_context(nc.psum_tensor("ps%d" % i, [128, 512], F32)), S.buf("ps%d" % i)))
        ptb = []
        for i in range(2):
            ptb.append((st.enter_context(nc.psum_tensor("pt%d" % i, [128, 1024], BF16)), S.buf("pt%d" % i)))
        b_mod_d = S.buf("mod_d")
        mod_rows(S, nc, st, c_col, ada_w, ada_b, mod_d, b_mod_d, 6 * D, ps)

        modc = sb("modc", [128, 96], F32)
        modj = sb("modj", [96, 128], F32)
        b_modc, b_modj = S.buf(), S.buf()
        S.dma("sp", modj[:], mod_d.rearrange("o (j p) -> (o j) p", p=128), reads=[b_mod_d], writes=[b_modj])
        gt1 = sb("gt1", [128, D], F32)
        gt2 = sb("gt2", [128, D], F32)
        b_gt1, b_gt2 = S.buf(), S.buf()
        S.dma("sp", gt1[:], mod_d[:, 2 * D:3 * D].partition_broadcast(128), reads=[b_mod_d], writes=[b_gt1])
        S.dma("sp", gt2[:], mod_d[:, 5 * D:6 * D].partition_broadcast(128), reads=[b_mod_d], writes=[b_gt2])
        g1c = sb("g1c", [128, KC], F32)
        g2c = sb("g2c", [128, KC], F32)
        gbc = sb("gbc", [128, 48], F32)
        b_g1c, b_g2c, b_gbc = S.buf(), S.buf(), S.buf()
        S.dma("sp", g1c[:], n1g, writes=[b_g1c])
        S.dma("sp", g2c[:], n2g, writes=[b_g2c])
        S.dma("sp", gbc[:], gate_b, writes=[b_gbc])
        scl1 = sb("scl1", [128, KC], F32)
        scl2 = sb("scl2", [128, KC], F32)
        b_scl1, b_scl2 = S.buf(), S.buf()
        idf = sb("idf", [128, 128], F32)
        idn = sb("idn", [128, 128], BF16)
        b_idf, b_id = S.buf(), S.buf()
        S.op("pool", lambda e: e.memset(idf[:], 0.0), writes=[b_idf])
        S.op("pool", lambda e: e.affine_select(out=idf[:], in_=idf[:], pattern=[[-1, 128]], compare_op=ALU.not_equal, fill=1.0, base=0, channel_multiplier=1), reads=[b_idf], writes=[b_idf])
        S.op("dve", lambda e: e.tensor_copy(idn[:], idf[:]), reads=[b_idf], writes=[b_id])
        S.op("pe", lambda e: e.transpose(ps[0][0][:, 0:96], modj[:], idf[0:96, 0:96]), reads=[b_modj, b_idf], writes=[ps[0][1]])
        S.op("dve", lambda e: e.tensor_copy(modc[:], ps[0][0][:, 0:96]), reads=[ps[0][1]], writes=[b_modc])
        S.op("dve", lambda e: e.scalar_tensor_tensor(scl1[:], modc[:, 16:32], 1.0, g1c[:], ALU.add, ALU.mult), reads=[b_modc, b_g1c], writes=[b_scl1])
        S.op("dve", lambda e: e.scalar_tensor_tensor(scl2[:], modc[:, 64:80], 1.0, g2c[:], ALU.add, ALU.mult), reads=[b_modc, b_g2c], writes=[b_scl2])

        dump(S, "modc", modc[:], [128, 96], F32, [b_modc])
        dump(S, "gt1", gt1[:], [128, D], F32, [b_gt1])
        dump(S, "scl1", scl1[:], [128, KC], F32, [b_scl1])
        dump(S, "idn", idn[:], [128, 128], BF16, [b_id])
        xt = sb("xt", [128, 4, D], F32); b_xt = [S.buf() for _ in range(4)]
        hT = sb("hT", [128, KC, 512], BF16); b_hT = [S.buf() for _ in range(4)]
        big = sb("big", [128, JC * 512], BF16)
        yT = big[:, 0:KC * 512].rearrange("p (k t) -> p k t", k=KC)
        mT = big[:, KC * 512:2 * KC * 512].rearrange("p (k t) -> p k t", k=KC)
        gT = big[:, :].rearrange("p (j t) -> p j t", j=JC)
        b_yT = S.buf(); b_mT = [S.buf() for _ in range(KC)]; b_gT = [S.buf() for _ in range(JC)]
        xn = sb("xn", [128, D], BF16); b_xn = S.buf()
        junk = xn; b_junk = b_xn
        ssq = sb("ssq", [128, 8], F32); b_ssq = S.buf()
        sig = [sb("sig%d" % i, [128, 512], F32) for i in range(2)]; b_sig = [S.buf() for _ in range(2)]
        tmp = [sb("tmp%d" % i, [128, 512], F32) for i in range(2)]; b_tmp = [S.buf() for _ in range(2)]
        macc = sb("macc", [128, 512], F32); b_macc = S.buf()
        NW = 3
        wr = [sb("wr%d" % i, [128, 8192], BF16) for i in range(NW)]; b_wr = [S.buf() for _ in range(NW)]
        wctr = [0]

        def wslot():
            i = wctr[0] % NW
            wctr[0] += 1
            return wr[i], b_wr[i]

        psi = [0]

        def nps():
            i = psi[0] % 6
            psi[0] += 1
            return ps[i]

        def norm_T(t0, s, sclc, biasc, first):
            S.op("act", lambda e: e.activation(junk[:], xt[:, s, :], AF.Square, accum_out=ssq[:, 0:1]), reads=[b_xt[s]], writes=[b_junk, b_ssq])
            S.op("dve", lambda e: e.tensor_scalar(ssq[:, 1:2], ssq[:, 0:1], 1.0 / D, EPS, ALU.mult, ALU.add), reads=[b_ssq], writes=[b_ssq])
            S.op("act", lambda e: e.sqrt(ssq[:, 2:3], ssq[:, 1:2]), reads=[b_ssq], writes=[b_ssq])
            S.op("dve", lambda e: e.reciprocal(ssq[:, 3:4], ssq[:, 2:3]), reads=[b_ssq], writes=[b_ssq])
            S.op("dve", lambda e: e.tensor_scalar_mul(xn[:], xt[:, s, :], ssq[:, 3:4]), reads=[b_xt[s], b_ssq], writes=[b_xn])
            for half in range(2):
                pt, bpt = ptb[half]
                for kk in range(8):
                    k = half * 8 + kk
                    S.op("pe", lambda e, pt=pt, kk=kk, k=k: e.transpose(pt[:, kk * 128:(kk + 1) * 128], xn[:, k * 128:(k + 1) * 128], idn[:]), reads=[b_xn, b_id], writes=[bpt])
                for kk in range(8):
                    k = half * 8 + kk
                    eng = "act" if kk % 2 == 0 else "dve"
                    if eng == "act":
                        S.op("act", lambda e, pt=pt, kk=kk, k=k: e.activation(hT[:, k, s * 128:(s + 1) * 128], pt[:, kk * 128:(kk + 1) * 128], AF.Identity, bias=biasc[:, k:k + 1], scale=sclc[:, k:k + 1]),
                             reads=[bpt, b_modc, b_scl1, b_scl2], writes=[b_hT[s]])
                    else:
                        S.op("dve", lambda e, pt=pt, kk=kk, k=k: e.tensor_scalar(hT[:, k, s * 128:(s + 1) * 128], pt[:, kk * 128:(kk + 1) * 128], sclc[:, k:k + 1], biasc[:, k:k + 1], ALU.mult, ALU.add),
                             reads=[bpt, b_modc, b_scl1, b_scl2], writes=[b_hT[s]])

        for ti in range(NT // 512):
            t0 = ti * 512
            for s in range(4):
                S.dma("sp", xt[:, s, :], x_d[t0 + s * 128:t0 + (s + 1) * 128, :], writes=[b_xt[s]])
            for k in range(KC):
                S.dma("act" if k % 2 else "sp", yT[:, k, :], y_d[t0:t0 + 512, k * 128:(k + 1) * 128], writes=[b_yT] + (b_gT if k == 0 else []), transpose=True)
            for s in range(4):
                norm_T(t0, s, scl1, modc[:, 0:16], ti == 0)
            if ti == 0:
                dump(S, "hT", hT[:], [128, KC, 512], BF16, b_hT)
                dump(S, "yT", yT, [128, KC, 512], BF16, [b_yT])
                dump(S, "ssq", ssq[:], [128, 8], F32, [b_ssq])
            for m in range(KC):
                wg, bwg = wslot()
                wgv = wg[:, 0:KC * 3 * 128].rearrange("p (k i c) -> p k i c", k=KC, i=3)
                wbv = wg[:, 6144:6144 + KC * 128].rearrange("p (k c) -> p k c", k=KC)
                S.dma("pool", wg[:], gwb[m], writes=[bwg])
                for i in range(3):
                    pg, bpg = nps()
                    for k in range(KC):
                        S.op("pe", lambda e, pg=pg, wgv=wgv, k=k, i=i: e.matmul(pg[:], wgv[:, k, i, :], hT[:, k, :], start=(k == 0), stop=(k == KC - 1)), reads=[bwg] + b_hT, writes=[bpg])
                    sg_, bsg = sig[i % 2], b_sig[i % 2]
                    S.op("act", lambda e, sg_=sg_, pg=pg, i=i, m=m: e.activation(sg_[:], pg[:], AF.Sigmoid, bias=gbc[:, i * 16 + m:i * 16 + m + 1]), reads=[bpg, b_gbc], writes=[bsg])
                    pb, bpb = nps()
                    k0, k1 = (0, 4) if i == 0 else ((4, 8) if i == 1 else (8, 16))
                    for k in range(k0, k1):
                        S.op("pe", lambda e, pb=pb, wbv=wbv, k=k, k0=k0, k1=k1: e.matmul(pb[:], wbv[:, k, :], yT[:, k, :], start=(k == k0), stop=(k == k1 - 1)), reads=[bwg, b_yT], writes=[bpb])
                    if i == 0:
                        S.op("dve", lambda e, sg_=sg_, pb=pb: e.tensor_mul(macc[:], sg_[:], pb[:]), reads=[bsg, bpb], writes=[b_macc])
                    elif i == 1:
                        tm, btm = tmp[0], b_tmp[0]
                        S.op("dve", lambda e, sg_=sg_, pb=pb, tm=tm: e.tensor_mul(tm[:], sg_[:], pb[:]), reads=[bsg, bpb], writes=[btm])
                        S.op("pool", lambda e, tm=tm: e.tensor_add(macc[:], macc[:], tm[:]), reads=[btm, b_macc], writes=[b_macc])
                    else:
                        tm, btm = tmp[1], b_tmp[1]
                        S.op("dve", lambda e, sg_=sg_, pb=pb, tm=tm: e.tensor_mul(tm[:], sg_[:], pb[:]), reads=[bsg, bpb], writes=[btm])
                        S.op("pool", lambda e, tm=tm, m=m: e.tensor_add(mT[:, m, :], macc[:], tm[:]), reads=[btm, b_macc], writes=[b_mT[m]])
            if ti == 0:
                dump(S, "mT", mT, [128, KC, 512], BF16, b_mT)
            for n in range(4):
                ww, bww = wslot()
                wv = ww[:, 0:KC * 512].rearrange("p (k c) -> p k c", k=KC)
                S.dma("pool", ww[:], wo_r[n], writes=[bww])
                for s in range(4):
                    po, bpo = nps()
                    for k in range(KC):
                        S.op("pe", lambda e, po=po, wv=wv, k=k, s=s: e.matmul(po[:], mT[:, k, s * 128:(s + 1) * 128], wv[:, k, :], start=(k == 0), stop=(k == KC - 1)), reads=[bww] + b_mT, writes=[bpo])
                    tm, btm = tmp[s % 2], b_tmp[s % 2]
                    S.op("dve", lambda e, tm=tm, po=po, n=n: e.tensor_mul(tm[:], po[:], gt1[:, n * 512:(n + 1) * 512]), reads=[bpo, b_gt1], writes=[btm])
                    S.op("pool", lambda e, tm=tm, s=s, n=n: e.tensor_add(xt[:, s, n * 512:(n + 1) * 512], xt[:, s, n * 512:(n + 1) * 512], tm[:]), reads=[btm, b_xt[s]], writes=[b_xt[s]])
            if ti == 0:
                dump(S, "x1", xt[:], [128, 4, D], F32, b_xt)
            for s in range(4):
                norm_T(t0, s, scl2, modc[:, 48:64], False)
            for j2 in range(JC // 2):
                ww, bww = wslot()
                w1v = ww[:, 0:KC * 256].rearrange("p (k c) -> p k c", k=KC)
                w3v = ww[:, 4096:4096 + KC * 256].rearrange("p (k c) -> p k c", k=KC)
                S.dma("pool", ww[:], w13[j2], writes=[bww])
                for jj in range(2):
                    j = j2 * 2 + jj
                    pa, bpa = nps()
                    for k in range(KC):
                        S.op("pe", lambda e, pa=pa, w1v=w1v, k=k, jj=jj: e.matmul(pa[:], w1v[:, k, jj * 128:(jj + 1) * 128], hT[:, k, :], start=(k == 0), stop=(k == KC - 1)), reads=[bww] + b_hT, writes=[bpa])
                    pb, bpb = nps()
                    for k in range(KC):
                        S.op("pe", lambda e, pb=pb, w3v=w3v, k=k, jj=jj: e.matmul(pb[:], w3v[:, k, jj * 128:(jj + 1) * 128], hT[:, k, :], start=(k == 0), stop=(k == KC - 1)), reads=[bww] + b_hT, writes=[bpb])
                    sg_, bsg = sig[j % 2], b_sig[j % 2]
                    S.op("act", lambda e, sg_=sg_, pa=pa: e.activation(sg_[:], pa[:], AF.Silu), reads=[bpa], writes=[bsg])
                    wr_ = [b_gT[j]] + ([b_yT] + b_mT if j == 0 else [])
                    S.op("dve", lambda e, sg_=sg_, pb=pb, j=j: e.tensor_mul(gT[:, j, :], sg_[:], pb[:]), reads=[bsg, bpb], writes=wr_)
            if ti == 0:
                dump(S, "h2T", hT[:], [128, KC, 512], BF16, b_hT)
                dump(S, "gT", gT, [128, JC, 512], BF16, b_gT)
            for n in range(4):
                pacc = [nps() for _ in range(4)]
                for pc in range(4):
                    ww, bww = wslot()
                    wv = ww[:, 0:11 * 512].rearrange("p (j c) -> p j c", j=11)
                    S.dma("pool", ww[:, 0:5632], w2r[n * 4 + pc], writes=[bww])
                    for s in range(4):
                        po, bpo = pacc[s]
                        for jj in range(11):
                            j = pc * 11 + jj
                            S.op("pe", lambda e, po=po, wv=wv, jj=jj, j=j, s=s: e.matmul(po[:], gT[:, j, s * 128:(s + 1) * 128], wv[:, jj, :], start=(j == 0), stop=(j == JC - 1)), reads=[bww, b_gT[j]], writes=[bpo])
                for s in range(4):
                    po, bpo = pacc[s]
                    tm, btm = tmp[s % 2], b_tmp[s % 2]
                    S.op("dve", lambda e, tm=tm, po=po, n=n: e.tensor_mul(tm[:], po[:], gt2[:, n * 512:(n + 1) * 512]), reads=[bpo, b_gt2], writes=[btm])
                    S.op("pool", lambda e, tm=tm, s=s, n=n: e.tensor_add(xt[:, s, n * 512:(n + 1) * 512], xt[:, s, n * 512:(n + 1) * 512], tm[:]), reads=[btm, b_xt[s]], writes=[b_xt[s]])
            for s in range(4):
                S.dma("sp", out_d[t0 + s * 128:t0 + (s + 1) * 128, :], xt[:, s, :], reads=[b_xt[s]])
        S.wait_all_dma("sp", b_xt)
        for name, bufs in dbg.items():
            S.wait_all_dma("sp", bufs)
        S.flush()
    return nc


def mixer_c(S, nc, L):
    SQ = L["SQ"]; NT128 = L["NT128"]; NT512 = L["NT512"]
    ps = L["ps"]; nps = L["nps"]; cst = L["cst"]; b_cst = L["b_cst"]
    cq = L["cq"]; b_cq = L["b_cq"]; bg = L["bg"]; b_bg = L["b_bg"]; zt = L["zt"]; b_zt = L["b_zt"]
    cn = L["cn"]; b_cn = L["b_cn"]; oc = L["oc"]; b_oc = L["b_oc"]; y_d = L["y_d"]; b_y = L["b_y"]
    convw_d = L["convw_d"]; gng_d = L["gng_d"]
    C = lambda i: cst[:, i, :]
    IDF, ONESF, ONESBD, NEGBD, TRIF, TRIB, NMI, NMIT, STR, STRT, ONESA, ONESB = [C(i) for i in range(12)]
    MA = cst[:, 12, 0:1]; MB = cst[:, 12, 1:2]
    R_ = [b_cst]
    CSUB = L.get("c_sub", 99)

    with ExitStack() as p1:
        sb = lambda name, shape, dtype: p1.enter_context(nc.sbuf_tensor("s_" + name, shape, dtype))
        cw = sb("cw", [128, 6, 5], F32); b_cw = S.buf()
        S.dma("sp", cw[:], convw_d, writes=[b_cw])
        xin = [sb("cxin%d" % i, [128, 516], F32) for i in range(3)]; b_xin = [S.buf() for _ in range(3)]
        acc = [sb("cacc%d" % i, [128, 512], F32) for i in range(3)]; b_acc = [S.buf() for _ in range(3)]
        sqf = [sb("csq%d" % i, [128, 512], F32) for i in range(2)]; b_sqf = [S.buf() for _ in range(2)]
        rsf = [sb("crs%d" % i, [128, 512], F32) for i in range(2)]; b_rsf = [S.buf() for _ in range(2)]
        it = 0; i2 = 0
        for hl in range(2):
            for ti in range(NT512):
                t0 = ti * 512
                for t in range(3):
                    a = it % 3; it += 1
                    ci = hl * 3 + t
                    S.dma("sp" if t % 2 else "act", xin[a][:], cq[ci, :, t0:t0 + 516], reads=[b_cq], writes=[b_xin[a]])
                    S.op("dve", lambda e, a=a, ci=ci: e.tensor_scalar_mul(acc[a][:], xin[a][:, 0:512], cw[:, ci, 0:1]), reads=[b_xin[a], b_cw], writes=[b_acc[a]])
                    for j in range(1, 5):
                        eng = "dve"
                        S.op(eng, lambda e, a=a, ci=ci, j=j: e.scalar_tensor_tensor(acc[a][:], xin[a][:, j:j + 512], cw[:, ci, j:j + 1], acc[a][:], ALU.mult, ALU.add), reads=[b_xin[a], b_cw, b_acc[a]], writes=[b_acc[a]])
                    S.op("act", lambda e, a=a: e.activation(acc[a][:], acc[a][:], AF.Silu), reads=[b_acc[a]], writes=[b_acc[a]])
                    if t < 2:
                        i = i2 % 2; i2 += 1
                        S.op("act", lambda e, a=a, i=i: e.activation(sqf[i][:], acc[a][:], AF.Square), reads=[b_acc[a]], writes=[b_sqf[i]])
                        pp, bpp = nps()
                        S.op("pe", lambda e, pp=pp, i=i: e.matmul(pp[:], ONESF, sqf[i][:], start=True, stop=True), reads=[b_cst, b_sqf[i]], writes=[bpp])
                        S.op("dve", lambda e, pp=pp, i=i: e.tensor_scalar_add(rsf[i][:], pp[:], EPS), reads=[bpp], writes=[b_rsf[i]])
                        S.op("act", lambda e, i=i: e.sqrt(rsf[i][:], rsf[i][:]), reads=[b_rsf[i]], writes=[b_rsf[i]])
                        S.op("dve", lambda e, i=i: e.reciprocal(rsf[i][:], rsf[i][:]), reads=[b_rsf[i]], writes=[b_rsf[i]])
                        sc = (128.0 ** -0.5) if t == 0 else 1.0
                        S.op("dve", lambda e, a=a, i=i, sc=sc: e.scalar_tensor_tensor(acc[a][:], acc[a][:], sc, rsf[i][:], ALU.mult, ALU.mult), reads=[b_acc[a], b_rsf[i]], writes=[b_acc[a]])
                    S.dma("sp", cn[ci, :, t0:t0 + 512], acc[a][:], reads=[b_acc[a]], writes=[b_cn])
        S.barrier()
    if CSUB <= 0:
        return

    with ExitStack() as p2:
        sb = lambda name, shape, dtype: p2.enter_context(nc.sbuf_tensor("s_" + name, shape, dtype))
        bgt = sb("bgt", [128, NT128, 8], F32); b_bgt = S.buf()
        bgv = bg.rearrange("(n p) c -> p n c", p=128)
        for n0 in range(0, NT128, 16):
            S.dma("sp", bgt[:, n0:n0 + 16, :], bgv[:, n0:n0 + 16, :], reads=[b_bg], writes=[b_bgt])
        gng = sb("gng", [128, 128], F32); b_gng = S.buf()
        S.dma("sp", gng[:], gng_d.partition_broadcast(128), writes=[b_gng])
        pc = sb("pc", [128, 8, NT128], F32); b_pc = S.buf()
        tmpc = sb("tmpc", [128, NT128], F32); b_tmpc = S.buf()
        St = sb("St", [128, 128], F32); b_St = S.buf()
        vnew = sb("vnew", [128, 128], F32); b_vn = S.buf()
        S.op("pool", lambda e: e.memset(vnew[:], 0.0), writes=[b_vn])
        NBUF = 2
        mk = lambda nm: ([sb("%s%d" % (nm, i), [128, 128], F32) for i in range(NBUF)], [S.buf() for _ in range(NBUF)])
        qT, b_qT = mk("qT"); kT, b_kT = mk("kT"); vT, b_vT = mk("vT")
        vb, b_vb = mk("vb"); kbg, b_kbg = mk("kbg"); ktA, b_ktA = mk("ktA"); ktB, b_ktB = mk("ktB")
        G1, b_G1 = mk("G1"); dec, b_dec = mk("dec"); decT, b_decT = mk("decT"); egm, b_egm = mk("egm")
        dT2, b_dT2 = mk("dT2"); d2, b_d2 = mk("d2")
        At, b_At = mk("At"); aiT, b_aiT = mk("aiT"); qgT, b_qgT = mk("qgT")
        Xa, b_Xa = mk("Xa"); Ya, b_Ya = mk("Ya")
        Pm, b_Pm = mk("Pm"); Rm, b_Rm = mk("Rm")
        u_, b_u = mk("u_"); wT, b_wT = mk("wT"); ot, b_ot = mk("ot")
        oprev, b_oprev = mk("oprev"); zs, b_zs = mk("zs"); yc, b_yc = ([sb("yc%d" % i, [128, 128], BF16) for i in range(2)], [S.buf() for _ in range(2)])
        fin = sb("fin", [128, 8], F32); b_fin = S.buf()
        junk = sb("cjunk", [128, 128], F32); b_junk = S.buf()
        Xb, b_Xb = mk("Xb"); Yb, b_Yb = mk("Yb")

        def copy_alt(n, out, in_, reads, writes):
            if n % 2 == 0:
                S.op("act", lambda e: e.copy(out, in_), reads=reads, writes=writes)
            else:
                S.op("dve", lambda e: e.tensor_copy(out, in_), reads=reads, writes=writes)

        pairno = 0
        for hl in range(2):
            for dr in range(2):
                TRI = TRIF if dr == 0 else TRIB
                NMD = NMI if dr == 0 else NMIT
                NMDT = NMIT if dr == 0 else NMI
                STRM = STR if dr == 0 else STRT
                cb = dr * 2 + hl
                S.op("dve", lambda e, cb=cb: e.tensor_copy(pc[:, 0, :], bgt[:, :, cb]), reads=[b_bgt], writes=[b_pc])
                S.op("dve", lambda e, cb=cb: e.tensor_copy(pc[:, 1, :], bgt[:, :, 4 + cb]), reads=[b_bgt], writes=[b_pc])
                pg, bpg = nps()
                S.op("pe", lambda e, pg=pg, TRI=TRI: e.matmul(pg[:, 0:NT128], TRI, pc[:, 1, :], start=True, stop=True), reads=[b_cst, b_pc], writes=[bpg])
                pl, bpl = nps()
                S.op("pe", lambda e, pl=pl: e.matmul(pl[:, 0:NT128], ONESBD, pc[:, 1, :], start=True, stop=True), reads=[b_cst, b_pc], writes=[bpl])
                pa, bpa = nps()
                S.op("pe", lambda e, pa=pa: e.matmul(pa[:, 0:NT128], ONESA, pc[:, 1, :], start=True, stop=True), reads=[b_cst, b_pc], writes=[bpa])
                pb, bpb = nps()
                S.op("pe", lambda e, pb=pb: e.matmul(pb[:, 0:NT128], ONESB, pc[:, 1, :], start=True, stop=True), reads=[b_cst, b_pc], writes=[bpb])
                S.op("act", lambda e, pg=pg: e.activation(pc[:, 2, :], pg[:, 0:NT128], AF.Exp), reads=[bpg], writes=[b_pc])
                S.op("dve", lambda e: e.tensor_mul(pc[:, 3, :], pc[:, 0, :], pc[:, 2, :]), reads=[b_pc], writes=[b_pc])
                S.op("act", lambda e, pg=pg: e.copy(tmpc[:], pg[:, 0:NT128]), reads=[bpg], writes=[b_tmpc])
                S.op("dve", lambda e, pl=pl: e.tensor_sub(tmpc[:], pl[:, 0:NT128], tmpc[:]), reads=[bpl, b_tmpc], writes=[b_tmpc])
                S.op("act", lambda e: e.activation(tmpc[:], tmpc[:], AF.Exp), reads=[b_tmpc], writes=[b_tmpc])
                S.op("dve", lambda e: e.tensor_scalar_mul(pc[:, 4, :], tmpc[:], MA), reads=[b_tmpc, b_cst], writes=[b_pc])
                S.op("dve", lambda e: e.tensor_scalar_mul(pc[:, 5, :], tmpc[:], MB), reads=[b_tmpc, b_cst], writes=[b_pc])
                S.op("act", lambda e, pa=pa: e.activation(pc[:, 6, :], pa[:, 0:NT128], AF.Exp), reads=[bpa], writes=[b_pc])
                S.op("act", lambda e, pb=pb: e.activation(pc[:, 7, :], pb[:, 0:NT128], AF.Exp), reads=[bpb], writes=[b_pc])
                S.op("pool", lambda e: e.memset(St[:], 0.0), writes=[b_St])
                order = list(range(NT128)) if dr == 0 else list(range(NT128 - 1, -1, -1))
                for p in order:
                    w = pairno % NBUF; pairno += 1
                    colp = lambda r, p=p: pc[:, r, p:p + 1]
                    for t, (dst, bd) in enumerate(((qT, b_qT), (kT, b_kT), (vT, b_vT))):
                        S.dma("sp" if t != 1 else "act", dst[w][:], cn[hl * 3 + t, :, p * 128:(p + 1) * 128], reads=[b_cn], writes=[bd[w]])
                    pk, bpk = nps()
                    S.op("pe", lambda e, pk=pk, w=w: e.transpose(pk[:, 0:128], kT[w][:], IDF), reads=[b_kT[w], b_cst], writes=[bpk])
                    pv, bpv = nps()
                    S.op("pe", lambda e, pv=pv, w=w: e.transpose(pv[:, 0:128], vT[w][:], IDF), reads=[b_vT[w], b_cst], writes=[bpv])
                    S.op("dve", lambda e, pv=pv, w=w, p=p: e.tensor_scalar_mul(vb[w][:], pv[:, 0:128], pc[:, 0, p:p + 1]), reads=[bpv, b_pc], writes=[b_vb[w]])
                    S.op("dve", lambda e, pk=pk, w=w, p=p: e.tensor_scalar_mul(kbg[w][:], pk[:, 0:128], pc[:, 3, p:p + 1]), reads=[bpk, b_pc], writes=[b_kbg[w]])
                    S.op("dve", lambda e, pk=pk, w=w, p=p: e.tensor_scalar_mul(ktA[w][:], pk[:, 0:128], pc[:, 4, p:p + 1]), reads=[bpk, b_pc], writes=[b_ktA[w]])
                    S.op("dve", lambda e, pk=pk, w=w, p=p: e.tensor_scalar_mul(ktB[w][:], pk[:, 0:128], pc[:, 5, p:p + 1]), reads=[bpk, b_pc], writes=[b_ktB[w]])
                    if CSUB <= 1:
                        continue
                    S.op("dve", lambda e, w=w, p=p, TRI=TRI: e.tensor_scalar_mul(G1[w][:], TRI, pc[:, 1, p:p + 1]), reads=[b_cst, b_pc], writes=[b_G1[w]])
                    if CSUB <= 1.2:
                        continue
                    pD, bpD = nps()
                    S.op("pe", lambda e, pD=pD, w=w: e.matmul(pD[:, 0:128], G1[w][:], ONESBD, start=True, stop=False), reads=[b_G1[w], b_cst], writes=[bpD])
                    S.op("pe", lambda e, pD=pD, w=w: e.matmul(pD[:, 0:128], NEGBD, G1[w][:], start=False, stop=True), reads=[b_G1[w], b_cst], writes=[bpD])
                    pDT, bpDT = nps()
                    S.op("pe", lambda e, pDT=pDT, w=w: e.matmul(pDT[:, 0:128], ONESBD, G1[w][:], start=True, stop=False), reads=[b_G1[w], b_cst], writes=[bpDT])
                    S.op("pe", lambda e, pDT=pDT, w=w: e.matmul(pDT[:, 0:128], G1[w][:], NEGBD, start=False, stop=True), reads=[b_G1[w], b_cst], writes=[bpDT])
                    pG, bpG = nps()
                    S.op("pe", lambda e, pG=pG, w=w: e.matmul(pG[:, 0:128], ONESF, G1[w][:], start=True, stop=True), reads=[b_G1[w], b_cst], writes=[bpG])
                    if CSUB <= 1.4:
                        continue
                    S.op("dve", lambda e, pD=pD, w=w, NMD=NMD: e.tensor_add(d2[w][:], pD[:, 0:128], NMD), reads=[bpD, b_cst], writes=[b_d2[w]])
                    if CSUB <= 1.6:
                        continue
                    S.op("act", lambda e, w=w: e.activation(dec[w][:], d2[w][:], AF.Exp), reads=[b_d2[w]], writes=[b_dec[w]])
                    if CSUB <= 1.7:
                        continue
                    S.op("dve", lambda e, pDT=pDT, w=w, NMDT=NMDT: e.tensor_add(dT2[w][:], pDT[:, 0:128], NMDT), reads=[bpDT, b_cst], writes=[b_dT2[w]])
                    if CSUB <= 1.8:
                        continue
                    S.op("act", lambda e, w=w: e.activation(decT[w][:], dT2[w][:], AF.Exp), reads=[b_dT2[w]], writes=[b_decT[w]])
                    if CSUB <= 1.9:
                        continue
                    S.op("act", lambda e, pG=pG, w=w: e.activation(egm[w][:], pG[:, 0:128], AF.Exp), reads=[bpG], writes=[b_egm[w]])
                    if CSUB <= 2:
                        continue
                    pKK, bpKK = nps()
                    S.op("pe", lambda e, pKK=pKK, w=w: e.matmul(pKK[:, 0:128], kT[w][:], kT[w][:], start=True, stop=True), reads=[b_kT[w]], writes=[bpKK])
                    S.op("dve", lambda e, pKK=pKK, w=w, p=p: e.scalar_tensor_tensor(At[w][:], pKK[:, 0:128], pc[:, 0, p:p + 1], dec[w][:], ALU.mult, ALU.mult), reads=[bpKK, b_pc, b_dec[w]], writes=[b_At[w]])
                    S.op("dve", lambda e, w=w, STRM=STRM: e.scalar_tensor_tensor(Ya[w][:], At[w][:], -1.0, STRM, ALU.mult, ALU.mult), reads=[b_At[w], b_cst], writes=[b_Ya[w]])
                    pKQ, bpKQ = nps()
                    S.op("pe", lambda e, pKQ=pKQ, w=w: e.matmul(pKQ[:, 0:128], kT[w][:], qT[w][:], start=True, stop=True), reads=[b_kT[w], b_qT[w]], writes=[bpKQ])
                    S.op("dve", lambda e, pKQ=pKQ, w=w: e.tensor_mul(aiT[w][:], pKQ[:, 0:128], decT[w][:]), reads=[bpKQ, b_decT[w]], writes=[b_aiT[w]])
                    S.op("pool", lambda e, w=w: e.tensor_mul(qgT[w][:], qT[w][:], egm[w][:]), reads=[b_qT[w], b_egm[w]], writes=[b_qgT[w]])
                    if CSUB <= 3:
                        continue
                    pX, bpX = nps()
                    S.op("pe", lambda e, pX=pX, w=w: e.transpose(pX[:, 0:128], Ya[w][:], IDF), reads=[b_Ya[w], b_cst], writes=[bpX])
                    S.op("dve", lambda e, pX=pX, w=w: e.tensor_copy(Xa[w][:], pX[:, 0:128]), reads=[bpX], writes=[b_Xa[w]])
                    S.op("dve", lambda e, pX=pX, w=w: e.tensor_add(Pm[w][:], pX[:, 0:128], IDF), reads=[bpX, b_cst], writes=[b_Pm[w]])
                    S.op("pool", lambda e, w=w: e.tensor_add(Rm[w][:], Ya[w][:], IDF), reads=[b_Ya[w], b_cst], writes=[b_Rm[w]])
                    Xc, bXc, Yc, bYc = Xa, b_Xa, Ya, b_Ya
                    Xn_, bXn_, Yn_, bYn_ = Xb, b_Xb, Yb, b_Yb
                    for i in range(5):
                        p1_, bp1 = nps()
                        S.op("pe", lambda e, p1_=p1_, w=w, Xc=Xc, Yc=Yc: e.matmul(p1_[:, 0:128], Yc[w][:], Xc[w][:], start=True, stop=True), reads=[bXc[w], bYc[w]], writes=[bp1])
                        if i < 4:
                            p2_, bp2 = nps()
                            S.op("pe", lambda e, p2_=p2_, w=w, Xc=Xc, Yc=Yc: e.matmul(p2_[:, 0:128], Xc[w][:], Yc[w][:], start=True, stop=True), reads=[bXc[w], bYc[w]], writes=[bp2])
                        S.op("dve", lambda e, p1_=p1_, w=w, Xn_=Xn_: e.tensor_copy(Xn_[w][:], p1_[:, 0:128]), reads=[bp1], writes=[bXn_[w]])
                        if i < 4:
                            S.op("dve", lambda e, p2_=p2_, w=w, Yn_=Yn_: e.tensor_copy(Yn_[w][:], p2_[:, 0:128]), reads=[bp2], writes=[bYn_[w]])
                        p3_, bp3 = nps()
                        S.op("pe", lambda e, p3_=p3_, w=w, Xn_=Xn_: e.matmul(p3_[:, 0:128], Rm[w][:], Xn_[w][:], start=True, stop=True), reads=[b_Rm[w], bXn_[w]], writes=[bp3])
                        if i < 4:
                            p4_, bp4 = nps()
                            S.op("pe", lambda e, p4_=p4_, w=w, Xn_=Xn_: e.matmul(p4_[:, 0:128], Xn_[w][:], Rm[w][:], start=True, stop=True), reads=[b_Rm[w], bXn_[w]], writes=[bp4])
                        S.op("dve", lambda e, p3_=p3_, w=w: e.tensor_add(Pm[w][:], Pm[w][:], p3_[:, 0:128]), reads=[bp3, b_Pm[w]], writes=[b_Pm[w]])
                        if i < 4:
                            S.op("dve", lambda e, p4_=p4_, w=w: e.tensor_add(Rm[w][:], Rm[w][:], p4_[:, 0:128]), reads=[bp4, b_Rm[w]], writes=[b_Rm[w]])
                        Xc, bXc, Yc, bYc, Xn_, bXn_, Yn_, bYn_ = Xn_, bXn_, Yn_, bYn_, Xc, bXc, Yc, bYc
                    if CSUB <= 4:
                        continue
                    pu, bpu = nps()
                    S.op("pe", lambda e, pu=pu, w=w: e.matmul(pu[:, 0:128], Pm[w][:], vb[w][:], start=True, stop=True), reads=[b_Pm[w], b_vb[w]], writes=[bpu])
                    S.op("dve", lambda e, pu=pu, w=w: e.tensor_copy(u_[w][:], pu[:, 0:128]), reads=[bpu], writes=[b_u[w]])
                    pw, bpw = nps()
                    S.op("pe", lambda e, pw=pw, w=w: e.matmul(pw[:, 0:128], kbg[w][:], Pm[w][:], start=True, stop=True), reads=[b_Pm[w], b_kbg[w]], writes=[bpw])
                    S.op("dve", lambda e, pw=pw, w=w: e.tensor_copy(wT[w][:], pw[:, 0:128]), reads=[bpw], writes=[b_wT[w]])
                    if CSUB <= 5:
                        continue
                    chunks = (0, 1) if dr == 0 else (1, 0)
                    for cx in chunks:
                        r0, r1 = cx * 64, cx * 64 + 64
                        kt_, bkt_ = (ktA, b_ktA) if cx == 0 else (ktB, b_ktB)
                        eglr = 6 + cx
                        pV, bpV = nps()
                        S.op("pe", lambda e, pV=pV, w=w: e.matmul(pV[:, 0:128], wT[w][:], St[:], start=True, stop=True), reads=[b_wT[w], b_St], writes=[bpV])
                        S.op("dve", lambda e, pV=pV, w=w, r0=r0, r1=r1: e.tensor_sub(vnew[r0:r1, :], u_[w][r0:r1, :], pV[r0:r1, 0:128]), reads=[bpV, b_u[w]], writes=[b_vn])
                        pO, bpO = nps()
                        S.op("pe", lambda e, pO=pO, w=w: e.matmul(pO[:, 0:128], qgT[w][:], St[:], start=True, stop=False), reads=[b_qgT[w], b_St], writes=[bpO])
                        S.op("pe", lambda e, pO=pO, w=w: e.matmul(pO[:, 0:128], aiT[w][:], vnew[:], start=False, stop=True), reads=[b_aiT[w], b_vn], writes=[bpO])
                        S.op("dve", lambda e, pO=pO, w=w, r0=r0, r1=r1: e.tensor_copy(ot[w][r0:r1, :], pO[r0:r1, 0:128]), reads=[bpO], writes=[b_ot[w]])
                        pS, bpS = nps()
                        S.op("pe", lambda e, pS=pS, w=w, kt_=kt_: e.matmul(pS[:, 0:128], kt_[w][:], vnew[:], start=True, stop=True), reads=[bkt_[w], b_vn], writes=[bpS])
                        S.op("dve", lambda e, pS=pS, p=p, eglr=eglr: e.scalar_tensor_tensor(St[:], St[:], pc[:, eglr, p:p + 1], pS[:, 0:128], ALU.mult, ALU.add), reads=[bpS, b_pc, b_St], writes=[b_St])
                    if CSUB <= 6:
                        continue
                    if dr == 0:
                        S.dma("sp", oc[hl, p * 128:(p + 1) * 128, :], ot[w][:], reads=[b_ot[w]], writes=[b_oc])
                    else:
                        S.dma("act", oprev[w][:], oc[hl, p * 128:(p + 1) * 128, :], reads=[b_oc], writes=[b_oprev[w]])
                        S.dma("act", zs[w][:], zt[p * 128:(p + 1) * 128, hl * 128:(hl + 1) * 128], reads=[b_zt], writes=[b_zs[w]])
                        S.op("pool", lambda e, w=w: e.tensor_add(ot[w][:], ot[w][:], oprev[w][:]), reads=[b_ot[w], b_oprev[w]], writes=[b_ot[w]])
                        S.op("dve", lambda e, w=w: e.tensor_mul(junk[:], ot[w][:], ot[w][:]), reads=[b_ot[w]], writes=[b_junk])
                        S.op("dve", lambda e: e.reduce_sum(fin[:, 0:1], junk[:], AX.X), reads=[b_junk], writes=[b_fin])
                        S.op("dve", lambda e: e.tensor_scalar(fin[:, 1:2], fin[:, 0:1], 1.0 / 128, EPS, ALU.mult, ALU.add), reads=[b_fin], writes=[b_fin])
                        S.op("act", lambda e: e.activation(fin[:, 2:3], fin[:, 1:2], AF.Ln), reads=[b_fin], writes=[b_fin])
                        S.op("act", lambda e: e.activation(fin[:, 3:4], fin[:, 2:3], AF.Exp, scale=-0.5), reads=[b_fin], writes=[b_fin])
                        S.op("dve", lambda e, w=w: e.scalar_tensor_tensor(ot[w][:], ot[w][:], fin[:, 3:4], gng[:], ALU.mult, ALU.mult), reads=[b_ot[w], b_fin, b_gng], writes=[b_ot[w]])
                        yi = p % 2
                        S.op("dve", lambda e, w=w, yi=yi: e.tensor_mul(yc[yi][:], ot[w][:], zs[w][:]), reads=[b_ot[w], b_zs[w]], writes=[b_yc[yi]])
                        S.dma("sp", y_d[p * 128:(p + 1) * 128, 256 + hl * 128:256 + (hl + 1) * 128], yc[yi][:], reads=[b_yc[yi]], writes=[b_y])
        S.barrier()


NCOL = 1800
NFM = 12
NTM = 264


def build_F(S_len, debug=False, do_c=True):
    SQ = S_len
    NT128 = SQ // 128
    NT512 = SQ // 512
    R = SQ // 64
    nc = bass.Bass("TRN2", target_bir_lowering=False)
    dt = nc.dram_tensor
    ein = lambda name, shape, dtype=F32: dt(name, shape, dtype, kind="ExternalInput").ap()
    x_d = ein("x", [SQ, D])
    c_col = ein("c_col", [128, KC])
    ada_w = ein("ada_w", [D, 2 * D])
    ada_b = ein("ada_b", [1, 2 * D])
    n1g = ein("n1g", [128, KC])
    w_in = ein("w_in", [128, KC, NCOL])
    gq_d = ein("gq", [128, 4])
    gsc_d = ein("gsc", [128, 4])
    aplan, NB = a_plan(R)
    biasA_d = ein("biasA", [128, NB, 128])
    tbb_d = ein("tbb", [128, 128])
    tba_d = ein("tba", [128, 128])
    bov_d = ein("bov", [128, 4, 512])
    aug_d = ein("aug", [4, 512])
    dal_d = ein("dal", [1, 256])
    lamc_d = ein("lamc", [128, 2])
    subg_d = ein("subg", [1, 128])
    convw_d = ein("convw", [128, 6, 5])
    alog_d = ein("alog", [1, 4])
    dtb_d = ein("dtb", [1, 4])
    gng_d = ein("gng", [1, 128])
    cst_d = ein("cst", [128, 13, 128])
    y_d = dt("y", [SQ, 512], BF16, kind="ExternalOutput").ap()
    mod_d = dt("mod_d", [1, 6 * D], F32).ap()
    qkA = dt("qkA", [3, 128, SQ], BF16).ap()
    qkB = dt("qkB", [3, 128, SQ], BF16).ap()
    cq = dt("cq", [6, 128, SQ + 4], F32).ap()
    zt = dt("zt", [SQ, 256], F32).ap()
    bg = dt("bg", [SQ, 8], F32).ap()
    cn = dt("cn", [6, 128, SQ], F32).ap()
    oc = dt("oc", [2, SQ, 128], F32).ap()
    b_qkA, b_qkB, b_cq, b_zt, b_bg, b_cn, b_oc, b_y = [None] * 8

    dbg = {}

    def dump(S, name, ap, shape, dtype, bufs):
        if not debug:
            return
        d = dt("dbg_" + name, shape, dtype, kind="ExternalOutput").ap()
        S.dma("sp", d, ap, reads=bufs)
        dbg[name] = bufs

    with ExitStack() as st:
        S = Sched(nc, st)
        b_qkA, b_qkB, b_cq, b_zt, b_bg, b_cn, b_oc, b_y = [S.buf(n) for n in ("qkA", "qkB", "cq", "zt", "bg", "cn", "oc", "y")]
        ps = []
        for i in range(7):
            ps.append((st.enter_context(nc.psum_tensor("ps%d" % i, [128, 512], F32)), S.buf("ps%d" % i)))
        ptb = (st.enter_context(nc.psum_tensor("ptb", [128, 1024], BF16)), S.buf("ptb"))
        psi = [0]

        def nps(n=7):
            i = psi[0] % n
            psi[0] += 1
            return ps[i]

        gsb = lambda name, shape, dtype: st.enter_context(nc.sbuf_tensor("s_" + name, shape, dtype))
        cst = gsb("cst_sb", [128, 13, 128], F32); b_cst = S.buf()
        S.dma("sp", cst[:], cst_d, writes=[b_cst])
        idf = cst[:, 0, :]
        idn = gsb("idn", [128, 128], BF16); b_id = S.buf()
        S.op("dve", lambda e: e.tensor_copy(idn[:], idf), reads=[b_cst], writes=[b_id])
        onesb = gsb("onesb", [128, 2, 128], BF16); b_onesb = S.buf()
        S.op("dve", lambda e: e.tensor_copy(onesb[:], cst[:, 1:3, :]), reads=[b_cst], writes=[b_onesb])

        with ExitStack() as p1:
            sb = lambda name, shape, dtype: p1.enter_context(nc.sbuf_tensor("s_" + name, shape, dtype))
            b_mod_d = S.buf("mod_d")
            mod_rows(S, nc, p1, c_col, ada_w, ada_b, mod_d, b_mod_d, 2 * D, ps)
            modc = sb("modc", [128, 32], F32); modj = sb("modj", [32, 128], F32)
            b_modc, b_modj = S.buf(), S.buf()
            S.dma("sp", modj[:], mod_d[:, 0:4096].rearrange("o (j p) -> (o j) p", p=128), reads=[b_mod_d], writes=[b_modj])
            S.op("pe", lambda e: e.transpose(ps[0][0][:, 0:32], modj[:], idf[0:32, 0:32]), reads=[b_modj, b_cst], writes=[ps[0][1]])
            S.op("dve", lambda e: e.tensor_copy(modc[:], ps[0][0][:, 0:32]), reads=[ps[0][1]], writes=[b_modc])
            g1c = sb("g1c", [128, KC], F32); b_g1c = S.buf()
            S.dma("sp", g1c[:], n1g, writes=[b_g1c])
            scl1 = sb("scl1", [128, KC], F32); b_scl1 = S.buf()
            S.op("dve", lambda e: e.scalar_tensor_tensor(scl1[:], modc[:, 16:32], 1.0, g1c[:], ALU.add, ALU.mult), reads=[b_modc, b_g1c], writes=[b_scl1])
            gq = sb("gq", [128, 4], F32); gsc = sb("gsc", [128, 4], F32); gs = sb("gs", [128, 4], F32)
            b_gq, b_gsc, b_gs = S.buf(), S.buf(), S.buf()
            S.dma("sp", gq[:], gq_d, writes=[b_gq])
            S.dma("sp", gsc[:], gsc_d, writes=[b_gsc])
            S.op("dve", lambda e: e.tensor_mul(gs[:], gq[:], gsc[:]), reads=[b_gq, b_gsc], writes=[b_gs])
            alg = sb("alg", [128, 4], F32); dtb = sb("dtb", [128, 4], F32); nega = sb("nega", [128, 4], F32)
            b_alg, b_dtb, b_nega = S.buf(), S.buf(), S.buf()
            S.dma("sp", alg[:], alog_d.partition_broadcast(128), writes=[b_alg])
            S.dma("sp", dtb[:], dtb_d.partition_broadcast(128), writes=[b_dtb])
            S.op("act", lambda e: e.activation(nega[:], alg[:], AF.Exp), reads=[b_alg], writes=[b_nega])
            S.op("dve", lambda e: e.tensor_scalar_mul(nega[:], nega[:], -1.0), reads=[b_nega], writes=[b_nega])
            wsb = sb("wsb", [128, KC, NCOL], BF16); b_w = S.buf()
            for k in range(KC):
                S.dma("pool", wsb[:, k, :], w_in[:, k, :], writes=[b_w])
            zpad = sb("zpad", [128, 2], F32); b_zp = S.buf()
            S.op("pool", lambda e: e.memset(zpad[:], 0.0), writes=[b_zp])
            for t in range(6):
                S.dma("sp", cq[t, :, 0:2], zpad[:], reads=[b_zp], writes=[b_cq])
                S.dma("sp", cq[t, :, SQ + 2:SQ + 4], zpad[:], reads=[b_zp], writes=[b_cq])

            xt = sb("xt", [128, 4, D], F32); b_xt = [S.buf() for _ in range(4)]
            hT = sb("hT", [128, KC, 512], BF16); b_hT = [S.buf() for _ in range(4)]
            xn = sb("xn", [128, D], BF16); b_xn = S.buf()
            ssq = sb("ssq", [128, 8], F32); b_ssq = S.buf()
            sq = [sb("sq%d" % i, [128, 512], BF16) for i in range(2)]; b_sq = [S.buf() for _ in range(2)]
            rs = [sb("rs%d" % i, [128, 512], F32) for i in range(2)]; b_rs = [S.buf() for _ in range(2)]
            ob = [sb("ob%d" % i, [128, 512], BF16) for i in range(3)]; b_ob = [S.buf() for _ in range(3)]
            of = [sb("of%d" % i, [128, 512], F32) for i in range(3)]; b_of = [S.buf() for _ in range(3)]
            tmo = [sb("tmo%d" % i, [128, NTM], F32) for i in range(2)]; b_tmo = [S.buf() for _ in range(2)]
            ctr = {"ob": 0, "of": 0, "sq": 0, "tm": 0}

            def norm_T(s):
                S.op("act", lambda e: e.activation(xn[:], xt[:, s, :], AF.Square, accum_out=ssq[:, 0:1]), reads=[b_xt[s]], writes=[b_xn, b_ssq])
                S.op("dve", lambda e: e.tensor_scalar(ssq[:, 1:2], ssq[:, 0:1], 1.0 / D, EPS, ALU.mult, ALU.add), reads=[b_ssq], writes=[b_ssq])
                S.op("act", lambda e: e.sqrt(ssq[:, 2:3], ssq[:, 1:2]), reads=[b_ssq], writes=[b_ssq])
                S.op("dve", lambda e: e.reciprocal(ssq[:, 3:4], ssq[:, 2:3]), reads=[b_ssq], writes=[b_ssq])
                S.op("dve", lambda e: e.tensor_scalar_mul(xn[:], xt[:, s, :], ssq[:, 3:4]), reads=[b_xt[s], b_ssq], writes=[b_xn])
                pt, bpt = ptb
                for half in range(2):
                    for kk in range(8):
                        k = half * 8 + kk
                        S.op("pe", lambda e, kk=kk, k=k: e.transpose(pt[:, kk * 128:(kk + 1) * 128], xn[:, k * 128:(k + 1) * 128], idn[:]), reads=[b_xn, b_id], writes=[bpt])
                    for kk in range(8):
                        k = half * 8 + kk
                        if kk % 2 == 0:
                            S.op("act", lambda e, kk=kk, k=k: e.activation(hT[:, k, s * 128:(s + 1) * 128], pt[:, kk * 128:(kk + 1) * 128], AF.Identity, bias=modc[:, k:k + 1], scale=scl1[:, k:k + 1]),
                                 reads=[bpt, b_modc, b_scl1], writes=[b_hT[s]])
                        else:
                            S.op("dve", lambda e, kk=kk, k=k: e.tensor_scalar(hT[:, k, s * 128:(s + 1) * 128], pt[:, kk * 128:(kk + 1) * 128], scl1[:, k:k + 1], modc[:, k:k + 1], ALU.mult, ALU.add),
                                 reads=[bpt, b_modc, b_scl1], writes=[b_hT[s]])

            for ti in range(NT512):
                t0 = ti * 512
                for s in range(4):
                    S.dma("sp", xt[:, s, :], x_d[t0 + s * 128:t0 + (s + 1) * 128, :], writes=[b_xt[s]])
                for s in range(4):
                    norm_T(s)
                for ch in range(NFM):
                    pu, bpu = nps()
                    for k in range(KC):
                        S.op("pe", lambda e, pu=pu, k=k, ch=ch: e.matmul(pu[:], wsb[:, k, ch * 128:(ch + 1) * 128], hT[:, k, :], start=(k == 0), stop=(k == KC - 1)), reads=[b_w] + b_hT, writes=[bpu])
                    if ch in (0, 1, 3, 4):
                        gi = {0: 0, 1: 1, 3: 2, 4: 3}[ch]
                        blk = 0 if ch < 2 else 1
                        dblk = 128.0 if ch < 2 else 64.0
                        i = ctr["sq"] % 2; ctr["sq"] += 1
                        S.op("act", lambda e, i=i, pu=pu: e.activation(sq[i][:], pu[:], AF.Square), reads=[bpu], writes=[b_sq[i]])
                        p2, bp2 = nps()
                        S.op("pe", lambda e, p2=p2, i=i, blk=blk: e.matmul(p2[:], onesb[:, blk, :], sq[i][:], start=True, stop=True), reads=[b_onesb, b_sq[i]], writes=[bp2])
                        S.op("dve", lambda e, p2=p2, i=i, dblk=dblk: e.tensor_scalar(rs[i][:], p2[:], 1.0 / dblk, EPS, ALU.mult, ALU.add), reads=[bp2], writes=[b_rs[i]])
                        S.op("act", lambda e, i=i: e.sqrt(rs[i][:], rs[i][:]), reads=[b_rs[i]], writes=[b_rs[i]])
                        S.op("dve", lambda e, i=i: e.reciprocal(rs[i][:], rs[i][:]), reads=[b_rs[i]], writes=[b_rs[i]])
                        o = ctr["ob"] % 3; ctr["ob"] += 1
                        S.op("dve", lambda e, o=o, pu=pu, gi=gi, i=i: e.scalar_tensor_tensor(ob[o][:], pu[:], gs[:, gi:gi + 1], rs[i][:], ALU.mult, ALU.mult), reads=[bpu, b_gs, b_rs[i]], writes=[b_ob[o]])
                        dst = (qkA if ch < 2 else qkB)[ch % 3, :, t0:t0 + 512]
                        S.dma("act", dst, ob[o][:], reads=[b_ob[o]], writes=[b_qkA if ch < 2 else b_qkB])
                    elif ch in (2, 5):
                        o = ctr["ob"] % 3; ctr["ob"] += 1
                        S.op("act", lambda e, o=o, pu=pu: e.copy(ob[o][:], pu[:]), reads=[bpu], writes=[b_ob[o]])
                        dst = (qkA if ch == 2 else qkB)[2, :, t0:t0 + 512]
                        S.dma("act", dst, ob[o][:], reads=[b_ob[o]], writes=[b_qkA if ch == 2 else b_qkB])
                    else:
                        o = ctr["of"] % 3; ctr["of"] += 1
                        S.op("act", lambda e, o=o, pu=pu: e.copy(of[o][:], pu[:]), reads=[bpu], writes=[b_of[o]])
                        S.dma("act", cq[ch - 6, :, 2 + t0:2 + t0 + 512], of[o][:], reads=[b_of[o]], writes=[b_cq])
                for s in range(4):
                    pz, bpz = nps()
                    for k in range(KC):
                        S.op("pe", lambda e, pz=pz, k=k, s=s: e.matmul(pz[:, 0:NTM], hT[:, k, s * 128:(s + 1) * 128], wsb[:, k, NFM * 128:NFM * 128 + NTM], start=(k == 0), stop=(k == KC - 1)), reads=[b_w, b_hT[s]], writes=[bpz])
                    i = ctr["tm"] % 2; ctr["tm"] += 1
                    tm, btm = tmo[i], b_tmo[i]
                    S.op("act", lambda e, tm=tm, pz=pz: e.activation(tm[:, 0:260], pz[:, 0:260], AF.Sigmoid), reads=[bpz], writes=[btm])
                    S.op("dve", lambda e, tm=tm, pz=pz: e.tensor_mul(tm[:, 0:256], tm[:, 0:256], pz[:, 0:256]), reads=[bpz, btm], writes=[btm])
                    S.op("dve", lambda e, tm=tm, pz=pz: e.tensor_add(tm[:, 260:264], pz[:, 260:264], dtb[:]), reads=[bpz, b_dtb], writes=[btm])
                    S.op("act", lambda e, tm=tm: e.activation(tm[:, 260:264], tm[:, 260:264], AF.Exp), reads=[btm], writes=[btm])
                    S.op("act", lambda e, tm=tm: e.activation(tm[:, 260:264], tm[:, 260:264], AF.Ln, bias=1.0), reads=[btm], writes=[btm])
                    S.op("dve", lambda e, tm=tm: e.tensor_mul(tm[:, 260:264], tm[:, 260:264], nega[:]), reads=[btm, b_nega], writes=[btm])
                    S.dma("sp", zt[t0 + s * 128:t0 + (s + 1) * 128, :], tm[:, 0:256], reads=[btm], writes=[b_zt])
                    S.dma("sp", bg[t0 + s * 128:t0 + (s + 1) * 128, :], tm[:, 256:264], reads=[btm], writes=[b_bg])
                if ti == 0:
                    dump(S, "hT", hT[:], [128, KC, 512], BF16, b_hT)
            S.barrier()
        dump(S, "qkA", qkA, [3, 128, SQ], BF16, [b_qkA])
        dump(S, "qkB", qkB, [3, 128, SQ], BF16, [b_qkB])
        dump(S, "cq", cq, [6, 128, SQ + 4], F32, [b_cq])
        dump(S, "zt", zt, [SQ, 256], F32, [b_zt])
        dump(S, "bg", bg, [SQ, 8], F32, [b_bg])

        def phase_A():
            with ExitStack() as p2:
                sb = lambda name, shape, dtype: p2.enter_context(nc.sbuf_tensor("s_" + name, shape, dtype))
                QT = sb("aQT", [128, SQ], BF16); KT = sb("aKT", [128, SQ], BF16); b_QT, b_KT = S.buf(), S.buf()
                VP = sb("aVP", [128, NT128, 144], BF16); b_VP = S.buf()
                bA = sb("bA", [128, NB, 128], F32); b_bA = S.buf()
                S.dma("sp", QT[:], qkA[0], reads=[b_qkA], writes=[b_QT])
                S.dma("act", KT[:], qkA[1], reads=[b_qkA], writes=[b_KT])
                S.dma("sp", bA[:], biasA_d, writes=[b_bA])
                S.op("pool", lambda e: e.memset(VP[:, :, 128:144], 1.0), writes=[b_VP])
                for t in range(NT128):
                    S.dma("sp" if t % 2 else "act", VP[:, t, 0:128], qkA[2, :, t * 128:(t + 1) * 128], reads=[b_qkA], writes=[b_VP], transpose=True)
                sc = [sb("asc%d" % i, [128, 128], F32) for i in range(3)]; b_sc = [S.buf() for _ in range(3)]
                pT = [sb("apT%d" % i, [128, 128], BF16) for i in range(3)]; b_pT = [S.buf() for _ in range(3)]
                rc = sb("arc", [128, 2], F32); b_rc = S.buf()
                ya = [sb("aya%d" % i, [128, 128], BF16) for i in range(2)]; b_ya = [S.buf() for _ in range(2)]
                it = 0
                for rp in range(R // 2):
                    po, bpo = ps[6]
                    ent = aplan[rp]
                    for n, (kt, bi) in enumerate(ent):
                        pq, bpq = nps(6)
                        S.op("pe", lambda e, pq=pq, kt=kt, rp=rp: e.matmul(pq[:, 0:128], KT[:, kt * 128:(kt + 1) * 128], QT[:, rp * 128:(rp + 1) * 128], start=True, stop=True), reads=[b_KT, b_QT], writes=[bpq])
                        i = it % 3; it += 1
                        S.op("dve", lambda e, i=i, pq=pq, bi=bi: e.tensor_add(sc[i][:], pq[:, 0:128], bA[:, bi, :]), reads=[bpq, b_bA], writes=[b_sc[i]])
                        S.op("act", lambda e, i=i: e.activation(pT[i][:], sc[i][:], AF.Exp), reads=[b_sc[i]], writes=[b_pT[i]])
                        S.op("pe", lambda e, i=i, kt=kt, n=n, ne=len(ent): e.matmul(po[:, 0:129], pT[i][:], VP[:, kt, 0:129], start=(n == 0), stop=(n == ne - 1)), reads=[b_pT[i], b_VP], writes=[bpo])
                    S.op("dve", lambda e: e.reciprocal(rc[:, 0:1], po[:, 128:129]), reads=[bpo], writes=[b_rc])
                    yi = rp % 2
                    S.op("dve", lambda e, yi=yi: e.tensor_scalar_mul(ya[yi][:], po[:, 0:128], rc[:, 0:1]), reads=[bpo, b_rc], writes=[b_ya[yi]])
                    S.dma("sp", y_d[rp * 128:(rp + 1) * 128, 0:128], ya[yi][:], reads=[b_ya[yi]], writes=[b_y])
                S.barrier()


        phase_A()
        def phase_B():
            with ExitStack() as p3:
                sb = lambda name, shape, dtype: p3.enter_context(nc.sbuf_tensor("s_" + name, shape, dtype))
                KB = [sb("bK%d" % c, [66, SQ], BF16) for c in range(2)]; b_KB = [S.buf(), S.buf()]
                VP = sb("bVP", [128, NT128, 144], BF16); b_VP = S.buf()
                onesrow = sb("onesrow", [66, 512], BF16); b_or = S.buf()
                S.op("pool", lambda e: e.memset(onesrow[:], 1.0), writes=[b_or])
                for c in range(2):
                    S.dma("sp", KB[c][0:64, :], qkB[1, c * 64:(c + 1) * 64, :], reads=[b_qkB], writes=[b_KB[c]])
                    for t in range(NT512):
                        S.dma("act", KB[c][64:66, t * 512:(t + 1) * 512], onesrow[64:66, :], reads=[b_or], writes=[b_KB[c]])
                S.op("pool", lambda e: e.memset(VP[:, :, 128:144], 1.0), writes=[b_VP])
                for t in range(NT128):
                    S.dma("sp" if t % 2 else "act", VP[:, t, 0:128], qkB[2, :, t * 128:(t + 1) * 128], reads=[b_qkB], writes=[b_VP], transpose=True)
                tbb = sb("tbb", [128, 128], F32); tba = sb("tba", [128, 128], F32); bov = sb("bov", [128, 4, 512], F32)
                b_tb = S.buf()
                S.dma("sp", tbb[:], tbb_d, writes=[b_tb]); S.dma("sp", tba[:], tba_d, writes=[b_tb]); S.dma("sp", bov[:], bov_d, writes=[b_tb])
                dal = sb("dal", [128, 256], F32); lamc = sb("lamc", [128, 2], F32); lw = sb("lw", [128, 128], F32); lam = sb("lam", [128, 8], F32)
                b_dal, b_lamc, b_lw, b_lam = S.buf(), S.buf(), S.buf(), S.buf()
                S.dma("sp", dal[:], dal_d.partition_broadcast(128), writes=[b_dal])
                S.dma("sp", lamc[:], lamc_d, writes=[b_lamc])
                S.op("dve", lambda e: e.tensor_mul(lw[:, 0:64], dal[:, 0:64], dal[:, 64:128]), reads=[b_dal], writes=[b_lw])
                S.op("dve", lambda e: e.tensor_mul(lw[:, 64:128], dal[:, 128:192], dal[:, 192:256]), reads=[b_dal], writes=[b_lw])
                S.op("dve", lambda e: e.reduce_sum(lam[:, 0:1], lw[:, 0:64], AX.X), reads=[b_lw], writes=[b_lam])
                S.op("dve", lambda e: e.reduce_sum(lam[:, 1:2], lw[:, 64:128], AX.X), reads=[b_lw], writes=[b_lam])
                S.op("act", lambda e: e.activation(lam[:, 2:4], lam[:, 0:2], AF.Exp), reads=[b_lam], writes=[b_lam])
                S.op("dve", lambda e: e.tensor_sub(lam[:, 4:5], lam[:, 2:3], lam[:, 3:4]), reads=[b_lam], writes=[b_lam])
                S.op("dve", lambda e: e.tensor_add(lam[:, 5:6], lam[:, 4:5], lamc[:, 0:1]), reads=[b_lam, b_lamc], writes=[b_lam])
                S.op("dve", lambda e: e.tensor_scalar_mul(lam[:, 6:7], lam[:, 5:6], -1.0), reads=[b_lam], writes=[b_lam])
                dump(S, "lam", lam[:], [128, 8], F32, [b_lam])
                subg = sb("subg", [128, 128], F32); b_subg = S.buf()
                S.dma("sp", subg[:], subg_d.partition_broadcast(128), writes=[b_subg])
                S.op("dve", lambda e: e.tensor_scalar_mul(subg[:], subg[:], lamc[:, 1:2]), reads=[b_subg, b_lamc], writes=[b_subg])
                QQ = [[[sb("bQ%d%d%d" % (c, sg, i), [66, 512], BF16) for i in range(2)] for sg in range(2)] for c in range(2)]
                b_QQ = [[[S.buf() for i in range(2)] for sg in range(2)] for c in range(2)]
                augf = sb("augf", [66, 2, 512], F32); b_augf = S.buf()
                for sg in range(2):
                    S.dma("sp", augf[64:66, sg, :], aug_d[sg * 2:sg * 2 + 2, :], writes=[b_augf])
                for c in range(2):
                    for sg in range(2):
                        for i in range(2):
                            S.op("dve", lambda e, c=c, sg=sg, i=i: e.tensor_copy(QQ[c][sg][i][64:66, :], augf[64:66, sg, :]), reads=[b_augf], writes=[b_QQ[c][sg][i]])
                pT = [sb("bpT%d" % i, [128, 512], BF16) for i in range(3)]; b_pT = [S.buf() for _ in range(3)]
                scf = [sb("bsc%d" % i, [128, 512], F32) for i in range(2)]; b_scf = [S.buf() for _ in range(2)]
                ob_ = [sb("bo%d" % i, [128, 128], F32) for i in range(2)]; b_ob_ = [S.buf() for _ in range(2)]
                oa = sb("boa", [128, 128], F32); b_oa = S.buf()
                rcb = sb("brc", [128, 8], F32); b_rcb = S.buf()
                yb = [sb("byb%d" % i, [128, 128], BF16) for i in range(2)]; b_yb = [S.buf() for _ in range(2)]
                junkb = sb("bjunk", [128, 128], F32); b_junkb = S.buf()
                def oacc(c, s):
                    n = c * 4 + s
                    t, b = ps[4 + n // 3]
                    o0 = (n % 3) * 132
                    return t[:, o0:o0 + 129], b, t, o0
                zrow = sb("zrow", [1, 512], BF16); b_zrow = S.buf()
                S.op("pool", lambda e: e.memset(zrow[:], 0.0), writes=[b_zrow])
                it = 0; si = 0
                for qt in range(NT512):
                    qi = qt % 2
                    for bk in range(3):
                        tz, bz = ps[4 + bk]
                        wdt = 396 if bk < 2 else 264
                        S.op("pe", lambda e, tz=tz, wdt=wdt: e.matmul(tz[:, 0:wdt], zrow[0:1, 0:128], zrow[0:1, 0:wdt], start=True, stop=False), reads=[b_zrow], writes=[bz])
                    for c in range(2):
                        for sg in range(2):
                            S.dma("sp" if sg else "act", QQ[c][sg][qi][0:64, :], qkB[0, c * 64:(c + 1) * 64, qt * 512:(qt + 1) * 512], reads=[b_qkB], writes=[b_QQ[c][sg][qi]])
                    for kt in range(NT128):
                        k0 = kt * 128; q0 = qt * 512
                        for c in range(2):
                            pq, bpq = nps(4)
                            i = it % 3; it += 1
                            if k0 + 128 <= q0:
                                dl = (q0 - k0) // 128
                                S.op("pe", lambda e, pq=pq, c=c, kt=kt, qi=qi: e.matmul(pq[:], KB[c][0:66, kt * 128:(kt + 1) * 128], QQ[c][0][qi][0:66, :], start=True, stop=True), reads=[b_KB[c], b_QQ[c][0][qi]], writes=[bpq])
                                S.op("act", lambda e, i=i, pq=pq, dl=dl: e.activation(pT[i][:], pq[:], AF.Exp, bias=tbb[:, dl:dl + 1]), reads=[bpq, b_tb], writes=[b_pT[i]])
                            elif k0 >= q0 + 512:
                                dl = (k0 - q0) // 128
                                S.op("pe", lambda e, pq=pq, c=c, kt=kt, qi=qi: e.matmul(pq[:], KB[c][0:66, kt * 128:(kt + 1) * 128], QQ[c][1][qi][0:66, :], start=True, stop=True), reads=[b_KB[c], b_QQ[c][1][qi]], writes=[bpq])
                                S.op("act", lambda e, i=i, pq=pq, dl=dl: e.activation(pT[i][:], pq[:], AF.Exp, bias=tba[:, dl:dl + 1]), reads=[bpq, b_tb], writes=[b_pT[i]])
                            else:
                                jj = (k0 - q0) // 128
                                S.op("pe", lambda e, pq=pq, c=c, kt=kt, qi=qi: e.matmul(pq[:], KB[c][0:64, kt * 128:(kt + 1) * 128], QQ[c][0][qi][0:64, :], start=True, stop=True), reads=[b_KB[c], b_QQ[c][0][qi]], writes=[bpq])
                                s2 = si % 2; si += 1
                                S.op("dve", lambda e, s2=s2, pq=pq, jj=jj: e.tensor_add(scf[s2][:], pq[:], bov[:, jj, :]), reads=[bpq, b_tb], writes=[b_scf[s2]])
                                S.op("act", lambda e, i=i, s2=s2: e.activation(pT[i][:], scf[s2][:], AF.Exp), reads=[b_scf[s2]], writes=[b_pT[i]])
                            for s in range(4):
                                oap, bo, _, _ = oacc(c, s)
                                S.op("pe", lambda e, oap=oap, i=i, s=s, kt=kt, c=c: e.matmul(oap, pT[i][:, s * 128:(s + 1) * 128], VP[:, kt, 0:129], start=False, stop=(kt == NT128 - 1 and (c * 4 + s) in (2, 5, 7))), reads=[b_pT[i], b_VP], writes=[bo])
                    for s in range(4):
                        for c in range(2):
                            oap, bo, t, o0 = oacc(c, s)
                            S.op("dve", lambda e, c=c, t=t, o0=o0: e.reciprocal(rcb[:, c:c + 1], t[:, o0 + 128:o0 + 129]), reads=[bo], writes=[b_rcb])
                            S.op("dve", lambda e, c=c, t=t, o0=o0: e.tensor_scalar_mul(ob_[c][:], t[:, o0:o0 + 128], rcb[:, c:c + 1]), reads=[bo, b_rcb], writes=[b_ob_[c]])
                        S.op("dve", lambda e: e.scalar_tensor_tensor(oa[:], ob_[1][:], lam[:, 6:7], ob_[0][:], ALU.mult, ALU.add), reads=b_ob_ + [b_lam], writes=[b_oa])
                        S.op("act", lambda e: e.activation(junkb[:], oa[:], AF.Square, accum_out=rcb[:, 2:3]), reads=[b_oa], writes=[b_junkb, b_rcb])
                        S.op("dve", lambda e: e.tensor_scalar(rcb[:, 3:4], rcb[:, 2:3], 1.0 / 128, EPS, ALU.mult, ALU.add), reads=[b_rcb], writes=[b_rcb])
                        S.op("act", lambda e: e.sqrt(rcb[:, 4:5], rcb[:, 3:4]), reads=[b_rcb], writes=[b_rcb])
                        S.op("dve", lambda e: e.reciprocal(rcb[:, 5:6], rcb[:, 4:5]), reads=[b_rcb], writes=[b_rcb])
                        yi = s % 2
                        S.op("dve", lambda e, yi=yi: e.scalar_tensor_tensor(yb[yi][:], oa[:], rcb[:, 5:6], subg[:], ALU.mult, ALU.mult), reads=[b_oa, b_rcb, b_subg], writes=[b_yb[yi]])
                        S.dma("sp", y_d[qt * 512 + s * 128:qt * 512 + (s + 1) * 128, 128:256], yb[yi][:], reads=[b_yb[yi]], writes=[b_y])
                S.barrier()


        phase_B()
        if do_c:
            mixer_c(S, nc, locals())

        S.wait_all_dma("sp", [b_y])
        for name, bufs in dbg.items():
            S.wait_all_dma("sp", bufs)
        S.flush()
    return nc


_CACHE = {}
SEQ_FULL = 16384


def _get_F():
    if "F" not in _CACHE:
        _CACHE["F"] = build_F(SEQ_FULL)
    return _CACHE["F"]


def _get_G():
    if "G" not in _CACHE:
        _CACHE["G"] = build_G(SEQ_FULL // 4)
    return _CACHE["G"]


def kernel(**inputs):
    inp = {k: np.asarray(v) for k, v in inputs.items()}
    x = np.ascontiguousarray(inp["x"], dtype=np.float32)
    B, S_, Dm = x.shape
    cvec = inp["c"]
    NTOK = S_ // 4
    for l in range(2):
        lam_init = 0.8 - 0.6 * math.exp(-0.3 * l)
        ncF = _get_F()
        in_maps = []
        for i in range(8):
            b, j = i // 4, i % 4
            m = prep_F(l, inp, j, S_, lam_init)
            m["x"] = x[b]
            m["c_col"] = col16(cvec[b])
            in_maps.append(m)
        res = run_bass_kernel_spmd(ncF, in_maps, core_ids=list(range(8)))
        yfull = np.zeros((B, S_, Dm), dtype=in_maps[0]["x"].dtype).astype(res.results[0]["y"].dtype)
        for i in range(8):
            b, j = i // 4, i % 4
            yi = np.asarray(res.results[i]["y"])
            yfull[b, :, j * 128:(j + 1) * 128] = yi[:, 0:128]
            yfull[b, :, 512 + j * 128:512 + (j + 1) * 128] = yi[:, 128:256]
            yfull[b, :, 1024 + 2 * j * 128:1024 + (2 * j + 2) * 128] = yi[:, 256:512]
        del res
        ncG = _get_G()
        W = prep_G_weights(l, inp)
        in_maps = []
        for i in range(8):
            b, t = i // 4, i % 4
            m = dict(W)
            m["x"] = np.ascontiguousarray(x[b, t * NTOK:(t + 1) * NTOK])
            m["y"] = np.ascontiguousarray(yfull[b, t * NTOK:(t + 1) * NTOK])
            m["c_col"] = col16(cvec[b])
            in_maps.append(m)
        res = run_bass_kernel_spmd(ncG, in_maps, core_ids=list(range(8)))
        xn = np.empty_like(x)
        for i in range(8):
            b, t = i // 4, i % 4
            xn[b, t * NTOK:(t + 1) * NTOK] = np.asarray(res.results[i]["out"])
        x = xn
        del res
    return x
```
